# Optimizing a Trainium2 kernel written in Bass

```python
import math
import jax
import jax.numpy as jnp
from jax import lax
import numpy as np

D_MODEL = 2048
BATCH = 8
SEQ = 2048
DEPTH = 2

GRID_W = 64
CTX_LEN = 256
EPS = 1e-6
N_MOD = 9
FFN_RESIDUAL = 0.5
D_FF = 5632

ATT_HEADS = 8
ATT_DH = 64
ATT_VD = 2 * ATT_DH
ATT_WIDTH = ATT_HEADS * ATT_VD
Q_BLOCK = 128
ROPE_BASE = 10000.0

SSM_INNER = 2048
SSM_HEADDIM = 64
SSM_HEADS = SSM_INNER // SSM_HEADDIM
SSM_GROUPS = 4
SSM_HPG = SSM_HEADS // SSM_GROUPS
SSM_STATE = 128
SSM_CHUNK = 128
CONV_W = 5
CONV_DIM = SSM_INNER + 2 * SSM_GROUPS * SSM_STATE
SSM_IN_COLS = SSM_INNER + CONV_DIM + 2 * SSM_HEADS

N_BRANCH = 2
IN_SPLITS = (ATT_WIDTH, 2 * ATT_WIDTH, 3 * ATT_WIDTH, 3 * ATT_WIDTH + SSM_IN_COLS)
IN_COLS = 3 * ATT_WIDTH + SSM_IN_COLS + N_BRANCH * D_MODEL

kernel_name = "hybrid_diffattn_ssd_dit_block"


def rms_norm(x, w):
    xf = x.astype(jnp.float32)
    y = xf * lax.rsqrt(jnp.mean(xf * xf, axis=-1, keepdims=True) + EPS)
    return (y * w.astype(jnp.float32)).astype(x.dtype)


def adaln_params(cond, w, b):
    m = jax.nn.silu(cond) @ w + b
    return m.reshape(cond.shape[0], 1, N_MOD, D_MODEL)


def modulated_norm(x, norm_w, mod, slot):
    shift, scale = mod[:, :, 3 * slot], mod[:, :, 3 * slot + 1]
    return rms_norm(x, norm_w) * (1.0 + scale) + shift


def swiglu(u, w_gu, w_down):
    g, v = jnp.split(u @ w_gu, 2, axis=-1)
    return (jax.nn.silu(g) * v) @ w_down


def ffn_sublayer(x, mod, slot, norm_w, w_gu, w_down):
    gate = mod[:, :, 3 * slot + 2]
    return x + FFN_RESIDUAL * gate * swiglu(modulated_norm(x, norm_w, mod, slot), w_gu, w_down)


def axial_rope(rows):
    n_freq = ATT_DH // 4
    inv = ROPE_BASE ** (-jnp.arange(n_freq, dtype=jnp.float32) / n_freq)
    row = jnp.repeat(jnp.arange(rows, dtype=jnp.float32), GRID_W)
    col = jnp.tile(jnp.arange(GRID_W, dtype=jnp.float32), rows)
    ang = jnp.concatenate([row[:, None] * inv, col[:, None] * inv], axis=-1)
    return jnp.cos(ang), jnp.sin(ang)


def apply_rope(t, cos, sin):
    tp = t.astype(jnp.float32).reshape(*t.shape[:-1], ATT_DH // 2, 2)
    cs = cos[None, :, None, None, :]
    sn = sin[None, :, None, None, :]
    t0, t1 = tp[..., 0], tp[..., 1]
    out = jnp.stack([t0 * cs - t1 * sn, t0 * sn + t1 * cs], axis=-1)
    return out.reshape(t.shape).astype(t.dtype)


def diff_attend(q, k, v, lam):
    s = jnp.einsum("bqhcd,bkhcd->bhcqk", q, k).astype(jnp.float32) * (ATT_DH ** -0.5)
    p = jax.nn.softmax(s, axis=-1)
    a = p[:, :, 0] - lam * p[:, :, 1]
    return jnp.einsum("bhqk,bkhe->bqhe", a.astype(v.dtype), v)


def latent_diff_attention(q_lat, k_all, v_all, lam):
    b, l = q_lat.shape[:2]
    nb = l // Q_BLOCK
    qb = jnp.moveaxis(q_lat.reshape(b, nb, Q_BLOCK, ATT_HEADS, 2, ATT_DH), 1, 0)
    o = lax.map(lambda qblk: diff_attend(qblk, k_all, v_all, lam), qb)
    return jnp.moveaxis(o, 0, 1).reshape(b, l, ATT_HEADS, ATT_VD)


def centred_dwconv(x, w, b):
    out = lax.conv_general_dilated(
        x, w[:, None, :].astype(x.dtype), window_strides=(1,),
        padding=((CONV_W // 2, CONV_W // 2),),
        dimension_numbers=("NWC", "WIO", "NWC"), feature_group_count=x.shape[-1])
    return out + b


def ssd_scan(xs, dt, a, bm, cm, h0):
    f32 = jnp.float32
    b, l = xs.shape[:2]
    nc = l // SSM_CHUNK
    x = xs.reshape(b, nc, SSM_CHUNK, SSM_GROUPS, SSM_HPG, SSM_HEADDIM).astype(f32)
    d = dt.reshape(b, nc, SSM_CHUNK, SSM_GROUPS, SSM_HPG).astype(f32)
    bc = bm.reshape(b, nc, SSM_CHUNK, SSM_GROUPS, SSM_STATE).astype(f32)
    cc = cm.reshape(b, nc, SSM_CHUNK, SSM_GROUPS, SSM_STATE).astype(f32)
    acum = jnp.cumsum(d * a, axis=2)
    xdt = x * d[..., None]
    tri = jnp.tril(jnp.ones((SSM_CHUNK, SSM_CHUNK), bool))[:, :, None, None]
    seg = acum[:, :, :, None] - acum[:, :, None, :]
    decay = jnp.where(tri, jnp.exp(jnp.where(tri, seg, 0.0)), 0.0)
    cb = jnp.einsum("bclgn,bcsgn->bclsg", cc, bc)
    y_diag = jnp.einsum("bclsg,bclsgr,bcsgrp->bclgrp", cb, decay, xdt)
    decay_end = jnp.exp(acum[:, :, -1:] - acum)
    states = jnp.einsum("bclgn,bclgr,bclgrp->bcgrpn", bc, decay_end, xdt)
    chunk_decay = jnp.exp(acum[:, :, -1])

    def step(h, inp):
        s, dec = inp
        return h * dec[..., None, None] + s, h

    h_fin, h_start = lax.scan(step, h0.astype(f32),
                              (jnp.moveaxis(states, 1, 0), jnp.moveaxis(chunk_decay, 1, 0)))
    h_start = jnp.moveaxis(h_start, 0, 1)
    y_off = jnp.einsum("bclgn,bcgrpn,bclgr->bclgrp", cc, h_start, jnp.exp(acum))
    y = (y_diag + y_off).reshape(b, l, SSM_GROUPS, SSM_HPG, SSM_HEADDIM)
    return y.astype(xs.dtype), h_fin


def ssm_prep(u, conv_w, conv_b, dt_bias):
    b, l = u.shape[:2]
    z, xbc, dt = jnp.split(u, [SSM_INNER, SSM_INNER + CONV_DIM], axis=-1)
    xbc = jax.nn.silu(centred_dwconv(xbc, conv_w, conv_b))
    xs, bm, cm = jnp.split(xbc, [SSM_INNER, SSM_INNER + SSM_GROUPS * SSM_STATE], axis=-1)
    xs = xs.reshape(b, l, SSM_GROUPS, SSM_HPG, SSM_HEADDIM)
    bm = bm.reshape(b, l, SSM_GROUPS, SSM_STATE)
    cm = cm.reshape(b, l, SSM_GROUPS, SSM_STATE)
    dt = jax.nn.softplus(dt.reshape(b, l, 2, SSM_GROUPS, SSM_HPG).astype(jnp.float32)
                         + dt_bias.reshape(2, SSM_GROUPS, SSM_HPG).astype(jnp.float32))
    return z, xs, bm, cm, dt


def bidir_ssd(xc, dtc, bc, cc, xl, dtl, bl, cl, a_log):
    a = -jnp.exp(a_log.astype(jnp.float32)).reshape(2, SSM_GROUPS, SSM_HPG)
    h0 = jnp.zeros((xc.shape[0], SSM_GROUPS, SSM_HPG, SSM_HEADDIM, SSM_STATE), jnp.float32)
    flip = lambda t: jnp.flip(t, axis=1)
    yc_f, hc_f = ssd_scan(xc, dtc[:, :, 0], a[0], bc, cc, h0)
    yl_f, _ = ssd_scan(xl, dtl[:, :, 0], a[0], bl, cl, hc_f)
    yc_b, hc_b = ssd_scan(flip(xc), flip(dtc[:, :, 1]), a[1], flip(bc), flip(cc), h0)
    yl_b, _ = ssd_scan(flip(xl), flip(dtl[:, :, 1]), a[1], flip(bl), flip(cl), hc_b)
    return yc_f + flip(yc_b), yl_f + flip(yl_b)


def ssm_finish(y, xs, z, d_skip, norm_w):
    b, l = z.shape[:2]
    y = y + d_skip.reshape(SSM_GROUPS, SSM_HPG)[..., None] * xs
    g = (y.reshape(b, l, SSM_INNER) * jax.nn.silu(z)).reshape(b, l, SSM_GROUPS, SSM_INNER // SSM_GROUPS)
    return rms_norm(g, norm_w.reshape(SSM_GROUPS, -1)).reshape(b, l, SSM_INNER)


def merge_branches(o_attn, o_ssm, gate_logits, w_ba, w_bs, w_out):
    g_attn, g_ssm = jnp.split(jax.nn.sigmoid(gate_logits), 2, axis=-1)
    return (g_attn * (o_attn @ w_ba) + g_ssm * (o_ssm @ w_bs)) @ w_out


def token_mixer(x_lat, x_ctx, mod_lat, mod_ctx, layer_idx, need_ctx, cos, sin,
                mix_norm, w_in, q_norm, k_norm, lq1, lk1, lq2, lk2, subln,
                conv_w, conv_b, dt_bias, a_log, d_skip, ssm_norm, w_ba, w_bs, w_out):
    b, l, _ = x_lat.shape
    lc = x_ctx.shape[1]
    p_lat = modulated_norm(x_lat, mix_norm, mod_lat, 1) @ w_in
    p_ctx = modulated_norm(x_ctx, mix_norm, mod_ctx, 1) @ w_in
    q_l, k_l, v_l, s_l, g_l = jnp.split(p_lat, IN_SPLITS, axis=-1)
    q_c, k_c, v_c, s_c, g_c = jnp.split(p_ctx, IN_SPLITS, axis=-1)

    lam_init = 0.8 - 0.6 * math.exp(-0.3 * layer_idx)
    f32 = jnp.float32
    lam = (jnp.exp(jnp.sum(lq1.astype(f32) * lk1.astype(f32)))
           - jnp.exp(jnp.sum(lq2.astype(f32) * lk2.astype(f32))) + lam_init)
    qk_shape = lambda t, n: t.reshape(b, n, ATT_HEADS, 2, ATT_DH)
    q_l = apply_rope(rms_norm(qk_shape(q_l, l), q_norm), cos, sin)
    k_l = apply_rope(rms_norm(qk_shape(k_l, l), k_norm), cos, sin)
    q_c = rms_norm(qk_shape(q_c, lc), q_norm)
    k_c = rms_norm(qk_shape(k_c, lc), k_norm)
    v_l = v_l.reshape(b, l, ATT_HEADS, ATT_VD)
    v_c = v_c.reshape(b, lc, ATT_HEADS, ATT_VD)
    k_all = jnp.concatenate([k_l, k_c], axis=1)
    v_all = jnp.concatenate([v_l, v_c], axis=1)
    att_post = lambda o: (rms_norm(o, subln) * (1.0 - lam_init)).reshape(o.shape[0], o.shape[1], ATT_WIDTH)
    o_attn_l = att_post(latent_diff_attention(q_l, k_all, v_all, lam))

    z_c, xs_c, bm_c, cm_c, dt_c = ssm_prep(s_c, conv_w, conv_b, dt_bias)
    z_l, xs_l, bm_l, cm_l, dt_l = ssm_prep(s_l, conv_w, conv_b, dt_bias)
    y_c, y_l = bidir_ssd(xs_c, dt_c, bm_c, cm_c, xs_l, dt_l, bm_l, cm_l, a_log)
    o_ssm_l = ssm_finish(y_l, xs_l, z_l, d_skip, ssm_norm)

    x_lat = x_lat + mod_lat[:, :, 5] * merge_branches(o_attn_l, o_ssm_l, g_l, w_ba, w_bs, w_out)
    if not need_ctx:
        return x_lat, None
    o_attn_c = att_post(diff_attend(q_c, k_c, v_c, lam))
    o_ssm_c = ssm_finish(y_c, xs_c, z_c, d_skip, ssm_norm)
    x_ctx = x_ctx + mod_ctx[:, :, 5] * merge_branches(o_attn_c, o_ssm_c, g_c, w_ba, w_bs, w_out)
    return x_lat, x_ctx


def setup_inputs(seed: int = 0) -> dict:
    key = jax.random.key(seed)
    ks = jax.random.split(key, 32)
    f32 = jnp.float32

    def nrm(k, shape, std=1.0):
        return std * jax.random.normal(k, shape, f32)

    def dense(k, shape, fan_in):
        return nrm(k, shape, fan_in ** -0.5)

    def gain(k, shape):
        return 1.0 + nrm(k, shape, 0.05)

    dt0 = jnp.exp(jax.random.uniform(ks[20], (DEPTH, 2, SSM_HEADS), f32, math.log(1e-3), math.log(1e-1)))
    return {
        "x": nrm(ks[0], (BATCH, SEQ, D_MODEL)),
        "c": nrm(ks[1], (BATCH, D_MODEL)),
        "ctx": nrm(ks[2], (BATCH, CTX_LEN, D_MODEL)),
        "c_ctx": nrm(ks[3], (D_MODEL,)),
        "ada_w": dense(ks[4], (DEPTH, D_MODEL, N_MOD * D_MODEL), D_MODEL),
        "ada_b": nrm(ks[5], (DEPTH, N_MOD * D_MODEL), 0.02),
        "ffn1_norm": gain(ks[6], (DEPTH, D_MODEL)),
        "ffn1_w_gu": dense(ks[7], (DEPTH, D_MODEL, 2 * D_FF), D_MODEL),
        "ffn1_w_down": dense(ks[8], (DEPTH, D_FF, D_MODEL), D_FF),
        "mix_norm": gain(ks[9], (DEPTH, D_MODEL)),
        "w_in": dense(ks[10], (DEPTH, D_MODEL, IN_COLS), D_MODEL),
        "q_norm": gain(ks[11], (DEPTH, ATT_DH)),
        "k_norm": gain(ks[12], (DEPTH, ATT_DH)),
        "lambda_q1": nrm(ks[13], (DEPTH, ATT_DH), 0.1),
        "lambda_k1": nrm(ks[14], (DEPTH, ATT_DH), 0.1),
        "lambda_q2": nrm(ks[15], (DEPTH, ATT_DH), 0.1),
        "lambda_k2": nrm(ks[16], (DEPTH, ATT_DH), 0.1),
        "attn_subln": gain(ks[17], (DEPTH, ATT_VD)),
        "conv_w": dense(ks[18], (DEPTH, CONV_W, CONV_DIM), CONV_W),
        "conv_b": nrm(ks[19], (DEPTH, CONV_DIM), 0.02),
        "dt_bias": dt0 + jnp.log(-jnp.expm1(-dt0)),
        "a_log": jnp.log(jax.random.uniform(ks[21], (DEPTH, 2, SSM_HEADS), f32, 1.0, 16.0)),
        "d_skip": 1.0 + nrm(ks[22], (DEPTH, SSM_HEADS), 0.1),
        "ssm_norm": gain(ks[23], (DEPTH, SSM_INNER)),
        "w_branch_attn": dense(ks[24], (DEPTH, ATT_WIDTH, D_MODEL), ATT_WIDTH),
        "w_branch_ssm": dense(ks[25], (DEPTH, SSM_INNER, D_MODEL), SSM_INNER),
        "w_out": dense(ks[26], (DEPTH, D_MODEL, D_MODEL), D_MODEL),
        "ffn2_norm": gain(ks[27], (DEPTH, D_MODEL)),
        "ffn2_w_gu": dense(ks[28], (DEPTH, D_MODEL, 2 * D_FF), D_MODEL),
        "ffn2_w_down": dense(ks[29], (DEPTH, D_FF, D_MODEL), D_FF),
    }


def reference(x, c, ctx, c_ctx, ada_w, ada_b, ffn1_norm, ffn1_w_gu, ffn1_w_down,
              mix_norm, w_in, q_norm, k_norm, lambda_q1, lambda_k1, lambda_q2, lambda_k2,
              attn_subln, conv_w, conv_b, dt_bias, a_log, d_skip, ssm_norm,
              w_branch_attn, w_branch_ssm, w_out, ffn2_norm, ffn2_w_gu, ffn2_w_down):
    l = x.shape[1]
    rows = l // GRID_W
    cos, sin = axial_rope(rows)
    for i in range(DEPTH):
        last = i == DEPTH - 1
        mod_l = adaln_params(c, ada_w[i], ada_b[i])
        mod_c = adaln_params(c_ctx[None], ada_w[i], ada_b[i])
        x = ffn_sublayer(x, mod_l, 0, ffn1_norm[i], ffn1_w_gu[i], ffn1_w_down[i])
        ctx = ffn_sublayer(ctx, mod_c, 0, ffn1_norm[i], ffn1_w_gu[i], ffn1_w_down[i])
        x, ctx_new = token_mixer(
            x, ctx, mod_l, mod_c, i, not last, cos, sin,
            mix_norm[i], w_in[i], q_norm[i], k_norm[i],
            lambda_q1[i], lambda_k1[i], lambda_q2[i], lambda_k2[i], attn_subln[i],
            conv_w[i], conv_b[i], dt_bias[i], a_log[i], d_skip[i], ssm_norm[i],
            w_branch_attn[i], w_branch_ssm[i], w_out[i])
        x = ffn_sublayer(x, mod_l, 2, ffn2_norm[i], ffn2_w_gu[i], ffn2_w_down[i])
        if not last:
            ctx = ffn_sublayer(ctx_new, mod_c, 2, ffn2_norm[i], ffn2_w_gu[i], ffn2_w_down[i])
    return x
```

```python
import contextlib
import math
import numpy as np
import concourse.bass as bass
import concourse.mybir as mybir
from concourse.bass_utils import run_bass_kernel_spmd

F32 = mybir.dt.float32
BF16 = mybir.dt.bfloat16
AF = mybir.ActivationFunctionType
ALU = mybir.AluOpType
AX = mybir.AxisListType

D = 2048
DFF = 5632
NJ = DFF // 128
NK = D // 128
HEADS = 8
EPS = 1e-6
IN_COLS = 12352
ENGINES = ("pe", "act", "dve", "pool", "sp")


class Buf:
    __slots__ = ("name", "last_w", "readers", "t", "key")

    def __init__(self, name, t=None, key=None):
        self.name = name
        self.last_w = None
        self.readers = []
        self.t = t
        self.key = key

    def __getitem__(self, idx):
        return self.t[idx]


class Op:
    __slots__ = ("eng", "fn", "deps", "dma", "key", "sig", "signal", "idx")


class Prog:
    def __init__(self, nc):
        self.nc = nc
        self.ops = []
        self.last_eng = {}
        self.last_dma = {}

    def op(self, eng, fn, reads=(), writes=(), dma=False, key=None):
        o = Op()
        o.eng, o.fn, o.dma, o.key = eng, fn, dma, key
        o.signal, o.sig, o.idx = False, None, len(self.ops)
        deps = set()
        for r in reads:
            if r.last_w is not None:
                deps.add(r.last_w)
        for w in writes:
            if w.last_w is not None:
                deps.add(w.last_w)
            lastr = {}
            for rd in w.readers:
                ro = self.ops[rd]
                lastr[("dma", ro.key) if ro.dma else ro.eng] = rd
            deps.update(lastr.values())
        o.deps = deps
        for r in reads:
            r.readers.append(o.idx)
        for w in writes:
            w.last_w = o.idx
            w.readers = []
        self.ops.append(o)
        self.last_eng[eng] = o.idx
        if dma:
            self.last_dma[key] = o.idx
        return o

    def barrier(self):
        deps = set(self.last_eng.values()) | set(self.last_dma.values())
        for e in ENGINES:
            o = Op()
            o.eng, o.fn, o.dma, o.key = e, None, False, None
            o.signal, o.sig, o.idx = False, None, len(self.ops)
            o.deps = set(deps)
            self.ops.append(o)
        self.last_dma = {}

    def finalize(self):
        nc, ops = self.nc, self.ops
        for o in ops:
            for d in o.deps:
                od = ops[d]
                if od.dma:
                    continue
                if od.eng == o.eng and od.eng == "pe" and not o.dma:
                    continue
                od.signal = True
        sem_names, counts = {}, {}
        for o in ops:
            if o.fn is None:
                continue
            if o.dma:
                k = ("dma", o.key)
                sem_names.setdefault(k, "d_%s" % o.key)
                counts[k] = counts.get(k, 0) + 16
                o.sig = (k, counts[k])
            elif o.signal:
                k = ("eng", o.eng)
                sem_names.setdefault(k, "e_" + o.eng)
                counts[k] = counts.get(k, 0) + 1
                o.sig = (k, counts[k])
        with contextlib.ExitStack() as st:
            sems = {k: st.enter_context(nc.semaphore(n)) for k, n in sem_names.items()}
            block = st.enter_context(nc.Block())
            per_eng = {e: [] for e in ENGINES}
            for o in ops:
                per_eng[o.eng].append(o)

            def make(engname):
                def body(eng):
                    known = {}
                    for o in per_eng[engname]:
                        waits = {}
                        for d in o.deps:
                            od = ops[d]
                            if od.sig is None:
                                continue
                            k, v = od.sig
                            if known.get(k, 0) >= v:
                                continue
                            if waits.get(k, 0) < v:
                                waits[k] = v
                        for k, v in waits.items():
                            eng.wait_ge(sems[k], v)
                            known[k] = v
                        if o.fn is None:
                            continue
                        ins = o.fn(eng)
                        if o.sig is not None:
                            ins.then_inc(sems[o.sig[0]], 16 if o.dma else 1)
                return body

            block.tensor(make("pe"))
            block.scalar(make("act"))
            block.vector(make("dve"))
            block.gpsimd(make("pool"))
            block.sync(make("sp"))


def build(L, LC, nlayers=2, debug=()):
    T = L + LC
    TT = 384
    assert T % TT == 0 and L % 128 == 0 and LC % 128 == 0
    PASS = min(768, T)
    assert T % PASS == 0
    NCH = T // 128
    QT_ = min(512, L)
    nc = bass.Bass("TRN2", target_bir_lowering=False)

    def din(name, shape, dt=F32):
        return nc.dram_tensor(name, list(shape), dt, kind="ExternalInput").ap()

    def dscr(name, shape, dt=F32):
        kind = "ExternalOutput" if (name in debug and dt == F32) else "Internal"
        return nc.dram_tensor(name, list(shape), dt, kind=kind).ap()

    xT = din("xT", [D, T])
    condT = din("condT", [128, NK, 2])
    ident_d = din("ident", [128, 128])
    masks_d = din("masks", [128, 6, 128])
    blk1_d = din("blockones", [128, 128])
    rmat_d = din("rmat", [128, 128])
    cos_d = din("cosT", [128, L])
    sin_d = din("sinT", [128, L])
    W = []
    for li in range(nlayers):
        s = str(li)
        W.append(dict(
            ada_w=din("ada_w" + s, [D, 9 * D]), ada_bT=din("ada_bT" + s, [128, 144]),
            normT=din("normT" + s, [128, 3, NK]),
            wgu=[din("wgu1_" + s, [NJ, 128, 2 * NK * 128]), din("wgu2_" + s, [NJ, 128, 2 * NK * 128])],
            wd=[din("wd1_" + s, [NK, 128, NJ * 128]), din("wd2_" + s, [NK, 128, NJ * 128])],
            w_in=din("w_in" + s, [D, IN_COLS]),
            qkn=din("qkn" + s, [128, 2]), lamv=din("lamv" + s, [128, 4, 64]), subln=din("subln" + s, [128, 1]),
            convw=din("convw" + s, [128, 24, 5]), convb=din("convb" + s, [128, 24]),
            dtb=din("dtb" + s, [128, 64]), alog=din("alog" + s, [128, 64]),
            dskip=din("dskip" + s, [128, D]), ssmn=din("ssmn" + s, [128, D]),
            w_ba=din("w_ba" + s, [NK, 128, 8 * 128]), w_bs=din("w_bs" + s, [NK, 128, NK * 128]),
            w_out=din("w_out" + s, [NK, 128, NK * 128]),
        ))
    yT = nc.dram_tensor("yT", [D, L], F32, kind="ExternalOutput").ap()
    XA = dscr("XA", [D, T])
    XB = dscr("XB", [D, T])
    QTs = dscr("QTs", [HEADS, 128, T], BF16)
    KTs = dscr("KTs", [HEADS, 128, T], BF16)
    Vtok = dscr("Vtok", [T, 1024], BF16)
    SZ = dscr("SZ", [T, D])
    XBC = dscr("XBC", [3072, T])
    DTDA = dscr("DTDA", [T, 128])
    GTs = dscr("GTs", [2 * D, T])
    XStok = dscr("XStok", [T, D])
    Btok = dscr("Btok", [T, 512], BF16)
    BTs = dscr("BTs", [4, 128, T], BF16)
    CTs = dscr("CTs", [4, 128, T], BF16)
    OATT = dscr("OATT", [1024, T], BF16)
    OSSM = dscr("OSSM", [D, T], BF16)

    with contextlib.ExitStack() as st:
        NW = 52800
        arena = st.enter_context(nc.sbuf_tensor("arena", [128, NW], F32))
        banks = [Buf("bank%d" % i, st.enter_context(nc.psum_tensor("bank%d" % i, [128, 512], F32))) for i in range(8)]
        P = Prog(nc)
        A = {"off": 0, "base": 0, "slot": 0}

        def alloc(shape, dt=F32, name="t"):
            n = int(np.prod(shape[1:]))
            words = n if dt == F32 else (n + 1) // 2
            off = A["off"]
            assert off + words <= NW, ("arena overflow", name, off, words)
            A["off"] = off + words
            v = arena[:, off:off + words]
            if dt != F32:
                v = v.bitcast(dt)
            if len(shape) == 3:
                v = v.rearrange("p (a b) -> p a b", a=shape[1])
            elif len(shape) == 4:
                v = v.rearrange("p (a b c) -> p a b c", a=shape[1], b=shape[2])
            A["slot"] += 1
            return Buf(name, v, key="s%d" % A["slot"])

        def phase_reset():
            P.barrier()
            A["off"] = A["base"]
            A["slot"] = A["pslot"]

        def MM(outb, out, lb, lhsT, rb, rhs, start, stop):
            P.op("pe", lambda e: e.matmul(out, lhsT=lhsT, rhs=rhs, start=start, stop=stop), [lb, rb], [outb])

        def TR(outb, out, ib, in_, idb, idap):
            P.op("pe", lambda e: e.transpose(out=out, in_=in_, identity=idap), [ib, idb], [outb])

        def ACT(outb, out, ib, in_, func, bias=0.0, scale=1.0, extra=(), accum=None, accb=None):
            w = [outb] + ([accb] if accb is not None else [])
            if accum is None:
                P.op("act", lambda e: e.activation(out=out, in_=in_, func=func, bias=bias, scale=scale), [ib] + list(extra), w)
            else:
                P.op("act", lambda e: e.activation(out=out, in_=in_, func=func, bias=bias, scale=scale, accum_out=accum), [ib] + list(extra), w)

        def TTo(eng, outb, out, ab, a, bb, b, op):
            P.op(eng, lambda e: e.tensor_tensor(out=out, in0=a, in1=b, op=op), [ab, bb], [outb])

        def TS(eng, outb, out, ab, a, s1, s2, op0, op1=None, extra=()):
            if op1 is None:
                P.op(eng, lambda e: e.tensor_scalar(out=out, in0=a, scalar1=s1, scalar2=None, op0=op0), [ab] + list(extra), [outb])
            else:
                P.op(eng, lambda e: e.tensor_scalar(out=out, in0=a, scalar1=s1, scalar2=s2, op0=op0, op1=op1), [ab] + list(extra), [outb])

        def STT(eng, outb, out, ab, a, sc, bb, b, op0, op1, extra=()):
            P.op(eng, lambda e: e.scalar_tensor_tensor(out=out, in0=a, scalar=sc, in1=b, op0=op0, op1=op1), [ab, bb] + list(extra), [outb])

        def CP(eng, outb, out, ib, in_):
            P.op(eng, lambda e: e.tensor_copy(out=out, in_=in_), [ib], [outb])

        def RECIP(outb, out, ib, in_):
            P.op("dve", lambda e: e.reciprocal(out=out, in_=in_), [ib], [outb])

        def LD(buf, out, src, q="sp"):
            P.op(q, lambda e: e.dma_start(out=out, in_=src), [], [buf], dma=True, key=buf.key)

        def STO(buf, dst, src, q="sp"):
            P.op(q, lambda e: e.dma_start(out=dst, in_=src), [buf], [], dma=True, key=buf.key)

        RR = ("act", "dve", "pool")

        class WS:
            def __init__(self, nst=6, words=2048, la=4):
                self.st = [alloc([128, words], F32, "wst%d" % i) for i in range(nst)]
                self.ch, self.nd, self.nc_, self.nst, self.la, self.marks = [], 0, 0, nst, la, []

            def add(self, dstb, dst, src, k):
                self.ch.append((dstb, dst, src, k))

            def mark(self):
                self.marks.append(len(self.ch))

            def _view(self, i):
                dstb, dst, src, k = self.ch[i]
                s_ = self.st[i % self.nst]
                n = int(np.prod(src.shape[1:]))
                sv = s_[:, 0:n]
                if k is not None:
                    sv = sv.rearrange("p (k m) -> p k m", k=k)
                return s_, sv

            def upto(self, kk):
                kk = min(kk, len(self.ch))
                while self.nc_ < kk:
                    while self.nd < min(self.nc_ + 1 + self.la, len(self.ch)):
                        s_, sv = self._view(self.nd)
                        LD(s_, sv, self.ch[self.nd][2], q="act")
                        self.nd += 1
                    i = self.nc_
                    dstb, dst, src, k = self.ch[i]
                    s_, sv = self._view(i)
                    eng = RR[i % 3]
                    if eng == "act":
                        ACT(dstb, dst, s_, sv, AF.Copy)
                    else:
                        CP(eng, dstb, dst, s_, sv)
                    self.nc_ += 1

            def unit(self, ui):
                self.upto(self.marks[min(ui + 1, len(self.marks) - 1)])

        def segs(t0, t1):
            out = []
            if t0 < L:
                out.append((t0, min(t1, L), 0))
            if t1 > L:
                out.append((max(t0, L), t1, 1))
            return out

        ident = alloc([128, 128], F32, "ident")
        masks = alloc([128, 6, 128], F32, "masks")
        onesb = alloc([128, 128], BF16, "onesb")
        onesf = alloc([128, 128], F32, "onesf")
        blk1 = alloc([128, 128], BF16, "blk1")
        blk1f = alloc([128, 128], F32, "blk1f")
        rmat = alloc([128, 128], F32, "rmat")
        condb = alloc([128, NK, 2], BF16, "condb")
        condf = alloc([128, NK, 2], F32, "condf")
        modT = alloc([128, 144, 2], F32, "modT")
        Gs = alloc([128, 3, NK, 2], F32, "Gs") if False else None
        Gs = [alloc([128, NK, 2], F32, "G%d" % s) for s in range(3)]
        GTg = [alloc([128, NK, 2], F32, "GT%d" % s) for s in range(3)]
        normT = alloc([128, 3, NK], F32, "normT")
        adab = alloc([128, 144], F32, "adab")
        qkn = alloc([128, 2], F32, "qkn")
        lamv = alloc([128, 4, 64], F32, "lamv")
        lamw = alloc([128, 8], F32, "lamw")
        subw = alloc([128, 1], F32, "subw")
        convw = alloc([128, 24, 5], F32, "convw")
        convb = alloc([128, 24], F32, "convb")
        dtb = alloc([128, 64], F32, "dtb")
        aneg = alloc([128, 64], F32, "aneg")
        A["base"] = A["off"]
        A["pslot"] = A["slot"]

        LD(ident, ident[:], ident_d[:, :])
        LD(masks, masks[:], masks_d[:, :, :])
        LD(blk1f, blk1f[:], blk1_d[:, :])
        LD(rmat, rmat[:], rmat_d[:, :])
        LD(condf, condf[:], condT[:, :, :])
        P.op("dve", lambda e: e.memset(onesf[:], 1.0), [], [onesf])
        P.op("dve", lambda e: e.memset(onesb[:], 1.0), [], [onesb])
        CP("dve", blk1, blk1[:], blk1f, blk1f[:])
        ACT(condb, condb[:], condf, condf[:], AF.Silu)

        def modnorm(xt, slot, g0, ub, uout, sqb, rs, rsd, bank):
            ACT(sqb, sqb[:], xt, xt[:], AF.Square)
            for k in range(NK):
                MM(bank, bank[:, 0:TT], onesb, onesb[:], sqb, sqb[:, k, :], k == 0, k == NK - 1)
            ACT(rs, rs[:], bank, bank[:, 0:TT], AF.Sqrt, bias=EPS, scale=1.0 / D)
            RECIP(rsd, rsd[:], rs, rs[:])
            for (a, b, i) in segs(g0, g0 + TT):
                a -= g0
                b -= g0
                n = b - a
                gb = Gs[slot][:, :, i:i + 1].to_broadcast([128, NK, n])
                sb_ = modT[:, slot * 48:slot * 48 + NK, i:i + 1].to_broadcast([128, NK, n])
                rb = rsd[:, a:b].unsqueeze(1).to_broadcast([128, NK, n])
                TTo("pool", xt, xt[:, :, a:b], xt, xt[:, :, a:b], Gs[slot], gb, ALU.mult)
                TTo("dve", xt, xt[:, :, a:b], xt, xt[:, :, a:b], rsd, rb, ALU.mult)
                TTo("dve", ub, uout[:, :, a:b], xt, xt[:, :, a:b], modT, sb_, ALU.add)

        def xview(Xd):
            return Xd.rearrange("(k p) t -> p k t", p=128)

        def phase_prep(li):
            w = W[li]
            phase_reset()
            LD(adab, adab[:], w["ada_bT"][:, :])
            LD(normT, normT[:], w["normT"][:, :, :])
            LD(qkn, qkn[:], w["qkn"][:, :])
            LD(lamv, lamv[:], w["lamv"][:, :, :])
            LD(subw, subw[:], w["subln"][:, :])
            LD(convw, convw[:], w["convw"][:, :, :])
            LD(convb, convb[:], w["convb"][:, :])
            LD(dtb, dtb[:], w["dtb"][:, :])
            LD(aneg, aneg[:], w["alog"][:, :])
            ACT(aneg, aneg[:], aneg, aneg[:], AF.Exp)
            TS("dve", aneg, aneg[:], aneg, aneg[:], -1.0, None, ALU.mult)
            lam_init = 0.8 - 0.6 * math.exp(-0.3 * li)
            tmp = alloc([128, 2, 64], F32, "lamtmp")
            TTo("dve", tmp, tmp[:, 0, :], lamv, lamv[:, 0, :], lamv, lamv[:, 1, :], ALU.mult)
            TTo("dve", tmp, tmp[:, 1, :], lamv, lamv[:, 2, :], lamv, lamv[:, 3, :], ALU.mult)
            P.op("dve", lambda e: e.reduce_sum(out=lamw[:, 0:2], in_=tmp[:], axis=AX.X), [tmp], [lamw])
            ACT(lamw, lamw[:, 2:4], lamw, lamw[:, 0:2], AF.Exp)
            TTo("dve", lamw, lamw[:, 4:5], lamw, lamw[:, 2:3], lamw, lamw[:, 3:4], ALU.subtract)
            TS("dve", lamw, lamw[:, 5:6], lamw, lamw[:, 4:5], lam_init, -1.0, ALU.add, ALU.mult)
            TS("dve", subw, subw[:], subw, subw[:], 1.0 - lam_init, None, ALU.mult)
            wb = [alloc([128, NK, 1024], BF16, "adaw%d" % i) for i in range(2)]
            pa = banks[0]
            ws = WS()
            for blk in range(18):
                wt = wb[blk % 2]
                for k0 in range(0, NK, 2):
                    ws.add(wt, wt[:, k0:k0 + 2, :], w["ada_w"][k0 * 128:(k0 + 2) * 128, blk * 1024:(blk + 1) * 1024].rearrange("(k p) n -> p k n", p=128), 2)
                ws.mark()
            for blk in range(18):
                wt = wb[blk % 2]
                ws.unit(blk)
                for mt in range(8):
                    j = blk * 8 + mt
                    for k in range(NK):
                        MM(pa, pa[:, 2 * j:2 * j + 2], wt, wt[:, k, mt * 128:(mt + 1) * 128], condb, condb[:, k, :], k == 0, k == NK - 1)
            TTo("dve", modT, modT[:], pa, pa[:, 0:288].rearrange("p (j i) -> p j i", i=2), adab,
                adab[:].unsqueeze(2).to_broadcast([128, 144, 2]), ALU.add)
            for s in range(3):
                TS("dve", Gs[s], Gs[s][:], modT, modT[:, (3 * s + 1) * NK:(3 * s + 2) * NK, :], 1.0, None, ALU.add)
                TTo("dve", Gs[s], Gs[s][:], Gs[s], Gs[s][:], normT, normT[:, s, :].unsqueeze(2).to_broadcast([128, NK, 2]), ALU.mult)
                TS("dve", GTg[s], GTg[s][:], modT, modT[:, (3 * s + 2) * NK:(3 * s + 3) * NK, :], 1.0 if s == 1 else 0.5, None, ALU.mult)

        def phase_ffn(li, s, Xin, Xout, final=False):
            w = W[li]
            slot = 0 if s == 0 else 2
            wgu, wd = w["wgu"][s], w["wd"][s]
            phase_reset()
            npt = PASS // TT
            u = alloc([128, NK, PASS], BF16, "u")
            hoff = A["off"]
            h = alloc([128, NJ, PASS], BF16, "h")
            hend = A["off"]
            A["off"] = hoff
            xts = [alloc([128, NK, TT], F32, "xt%d" % i) for i in range(1)] * 2
            sqb = alloc([128, NK, TT], BF16, "sqb")
            A["off"] = hend
            rs = alloc([128, TT], F32, "rs")
            rsd = alloc([128, TT], F32, "rsd")
            wg = [alloc([128, 2, NK, 128], BF16, "wg%d" % i) for i in range(2)]
            wdn = [alloc([128, NJ, 128], BF16, "wd%d" % i) for i in range(2)]
            sg = [alloc([128, TT], F32, "sg%d" % i) for i in range(2)]
            xm = [alloc([128, TT], F32, "xm%d" % i) for i in range(2)]
            xo = [alloc([128, TT], F32, "xo%d" % i) for i in range(2)]
            cnt = 0
            ws = WS()
            for p in range(T // PASS):
                for j in range(NJ):
                    for a_ in range(2):
                        ws.add(wg[j % 2], wg[j % 2][:, a_, :, :], wgu[j][:, a_ * 2048:(a_ + 1) * 2048], NK)
                    ws.mark()
                for m in range(NK):
                    for k0 in range(0, NJ, 16):
                        k1 = min(NJ, k0 + 16)
                        ws.add(wdn[m % 2], wdn[m % 2][:, k0:k1, :], wd[m][:, k0 * 128:k1 * 128], k1 - k0)
                    ws.mark()
            ui = 0
            for p in range(T // PASS):
                t0 = p * PASS
                P.barrier()
                for tt in range(npt):
                    g0 = t0 + tt * TT
                    xt = xts[tt % 2]
                    LD(xt, xt[:], xview(Xin)[:, :, g0:g0 + TT])
                    modnorm(xt, slot, g0, u, u[:, :, tt * TT:(tt + 1) * TT], sqb, rs, rsd, banks[6])
                P.barrier()
                for j in range(NJ):
                    wt = wg[j % 2]
                    ws.unit(ui)
                    ui += 1
                    for tt in range(npt):
                        gp, vp = banks[(cnt % 2) * 2], banks[(cnt % 2) * 2 + 1]
                        sgt = sg[cnt % 2]
                        cnt += 1
                        cs = slice(tt * TT, (tt + 1) * TT)
                        for k in range(NK):
                            MM(gp, gp[:, 0:TT], wt, wt[:, 0, k, :], u, u[:, k, cs], k == 0, k == NK - 1)
                        for k in range(NK):
                            MM(vp, vp[:, 0:TT], wt, wt[:, 1, k, :], u, u[:, k, cs], k == 0, k == NK - 1)
                        ACT(sgt, sgt[:], gp, gp[:, 0:TT], AF.Silu)
                        TTo("dve", h, h[:, j, cs], sgt, sgt[:], vp, vp[:, 0:TT], ALU.mult)
                for m in range(NK):
                    wt = wdn[m % 2]
                    ws.unit(ui)
                    ui += 1
                    for tt in range(npt):
                        g0 = t0 + tt * TT
                        op_ = banks[4 + cnt % 2]
                        xmt, xot = xm[cnt % 2], xo[cnt % 2]
                        cnt += 1
                        cs = slice(tt * TT, (tt + 1) * TT)
                        LD(xmt, xmt[:], Xin[m * 128:(m + 1) * 128, g0:g0 + TT])
                        for k in range(NJ):
                            MM(op_, op_[:, 0:TT], wt, wt[:, k, :], h, h[:, k, cs], k == 0, k == NJ - 1)
                        for (a, b, i) in segs(g0, g0 + TT):
                            STT("dve", xot, xot[:, a - g0:b - g0], op_, op_[:, a - g0:b - g0], GTg[slot][:, m, i:i + 1],
                                xmt, xmt[:, a - g0:b - g0], ALU.mult, ALU.add, extra=[GTg[slot]])
                        if not final:
                            STO(xot, Xout[m * 128:(m + 1) * 128, g0:g0 + TT], xot[:])
                        else:
                            for (a, b, i) in segs(g0, g0 + TT):
                                if i == 0:
                                    STO(xot, yT[m * 128:(m + 1) * 128, a:b], xot[:, a - g0:b - g0])

        def phase_proj(li, Xin):
            w = W[li]
            phase_reset()
            u = alloc([128, NK, T], BF16, "u")
            off0 = A["off"]
            xts = [alloc([128, NK, TT], F32, "xt%d" % i) for i in range(1)] * 2
            sqb = alloc([128, NK, TT], BF16, "sqb")
            rs = alloc([128, TT], F32, "rs")
            rsd = alloc([128, TT], F32, "rsd")
            for tt in range(T // TT):
                g0 = tt * TT
                xt = xts[tt % 2]
                LD(xt, xt[:], xview(Xin)[:, :, g0:g0 + TT])
                modnorm(xt, 1, g0, u, u[:, :, g0:g0 + TT], sqb, rs, rsd, banks[6])
            P.barrier()
            A["off"] = off0
            wbk = [alloc([128, NK, 512], BF16, "wb%d" % i) for i in range(2)]
            cosT = alloc([128, L], F32, "cosT")
            sinT = alloc([128, L], F32, "sinT")
            LD(cosT, cosT[:], cos_d[:, :])
            LD(sinT, sinT[:], sin_d[:, :])
            stg = [alloc([128, 512], F32, "stg%d" % i) for i in range(2)]
            stgb = [alloc([128, 512], BF16, "stgb%d" % i) for i in range(2)]
            qf = alloc([128, TT], F32, "qf")
            qn = alloc([128, TT], F32, "qn")
            sq2 = alloc([128, TT], BF16, "sq2")
            r1 = alloc([128, TT], F32, "r1")
            r2 = alloc([128, TT], F32, "r2")
            t1 = alloc([128, TT], F32, "t1")
            t2 = alloc([128, TT], F32, "t2")
            st = {"n": 0, "b": 0}
            ws = WS()
            blocks = ([(i * 512, 512) for i in range(16)] + [(8192, 64)] + [(8256 + i * 512, 512) for i in range(8)])
            for bi, (c0, ncols) in enumerate(blocks):
                wt = wbk[bi % 2]
                for k0 in range(0, NK, 4):
                    ws.add(wt, wt[:, k0:k0 + 4, 0:ncols], w["w_in"][k0 * 128:(k0 + 4) * 128, c0:c0 + ncols].rearrange("(k p) n -> p k n", p=128), 4)
                ws.mark()

            def load_block(c0, ncols):
                bi = st["b"]
                assert blocks[bi] == (c0, ncols), (bi, c0, ncols)
                wt = wbk[bi % 2]
                ws.unit(bi)
                st["b"] += 1
                return wt

            def fm_tile(wt, mt, g0):
                bk = banks[st["n"] % 2]
                st["n"] += 1
                for k in range(NK):
                    MM(bk, bk[:, 0:TT], wt, wt[:, k, mt * 128:(mt + 1) * 128], u, u[:, k, g0:g0 + TT], k == 0, k == NK - 1)
                return bk

            def tm_tile(wt, c, ncols):
                bk = banks[st["n"] % 2]
                st["n"] += 1
                for k in range(NK):
                    MM(bk, bk[:, 0:ncols], u, u[:, k, c * 128:(c + 1) * 128], wt, wt[:, k, 0:ncols], k == 0, k == NK - 1)
                return bk

            for which, dst in ((0, QTs), (1, KTs)):
                for blk in range(2):
                    wt = load_block(which * 1024 + blk * 512, 512)
                    for mt in range(4):
                        hd = blk * 4 + mt
                        for tt in range(T // TT):
                            g0 = tt * TT
                            bk = fm_tile(wt, mt, g0)
                            ACT(qf, qf[:], bk, bk[:, 0:TT], AF.Copy)
                            ACT(sq2, sq2[:], bk, bk[:, 0:TT], AF.Square)
                            MM(banks[2], banks[2][:, 0:TT], blk1, blk1[:], sq2, sq2[:], True, True)
                            ACT(r1, r1[:], banks[2], banks[2][:, 0:TT], AF.Sqrt, bias=EPS, scale=1.0 / 64)
                            RECIP(r2, r2[:], r1, r1[:])
                            STT("dve", qn, qn[:], qf, qf[:], qkn[:, which:which + 1], r2, r2[:], ALU.mult, ALU.mult, extra=[qkn])
                            ob = stgb[st["n"] % 2]
                            for (a, b, i) in segs(g0, g0 + TT):
                                la, lb = a - g0, b - g0
                                if i == 0:
                                    MM(banks[3], banks[3][:, la:lb], rmat, rmat[:], qn, qn[:, la:lb], True, True)
                                    TTo("pool", t1, t1[:, la:lb], qn, qn[:, la:lb], cosT, cosT[:, a:b], ALU.mult)
                                    TTo("dve", t2, t2[:, la:lb], banks[3], banks[3][:, la:lb], sinT, sinT[:, a:b], ALU.mult)
                                    TTo("dve", ob, ob[:, la:lb], t1, t1[:, la:lb], t2, t2[:, la:lb], ALU.add)
                                else:
                                    CP("dve", ob, ob[:, la:lb], qn, qn[:, la:lb])
                            STO(ob, dst[hd, :, g0:g0 + TT], ob[:, 0:TT])
            for blk in range(2):
                wt = load_block(2048 + blk * 512, 512)
                for c in range(NCH):
                    bk = tm_tile(wt, c, 512)
                    ob = stgb[st["n"] % 2]
                    ACT(ob, ob[:], bk, bk[:], AF.Copy)
                    STO(ob, Vtok[c * 128:(c + 1) * 128, blk * 512:(blk + 1) * 512], ob[:])
            for blk in range(4):
                wt = load_block(3072 + blk * 512, 512)
                for c in range(NCH):
                    bk = tm_tile(wt, c, 512)
                    ob = stg[st["n"] % 2]
                    ACT(ob, ob[:], bk, bk[:], AF.Silu)
                    STO(ob, SZ[c * 128:(c + 1) * 128, blk * 512:(blk + 1) * 512], ob[:])
            for blk in range(6):
                wt = load_block(5120 + blk * 512, 512)
                for mt in range(4):
                    for tt in range(T // TT):
                        g0 = tt * TT
                        bk = fm_tile(wt, mt, g0)
                        ob = stg[st["n"] % 2]
                        ACT(ob, ob[:, 0:TT], bk, bk[:, 0:TT], AF.Copy)
                        STO(ob, XBC[(blk * 4 + mt) * 128:(blk * 4 + mt + 1) * 128, g0:g0 + TT], ob[:, 0:TT])
            wt = load_block(8192, 64)
            for c in range(NCH):
                bk = tm_tile(wt, c, 64)
                ob = stg[st["n"] % 2]
                TTo("dve", ob, ob[:, 128:192], bk, bk[:, 0:64], dtb, dtb[:], ALU.add)
                ACT(ob, ob[:, 192:256], ob, ob[:, 128:192], AF.Exp)
                ACT(ob, ob[:, 0:64], ob, ob[:, 192:256], AF.Ln, bias=1.0)
                TTo("dve", ob, ob[:, 64:128], ob, ob[:, 0:64], aneg, aneg[:], ALU.mult)
                STO(ob, DTDA[c * 128:(c + 1) * 128, :], ob[:, 0:128])
            for blk in range(8):
                wt = load_block(8256 + blk * 512, 512)
                for mt in range(4):
                    for tt in range(T // TT):
                        g0 = tt * TT
                        bk = fm_tile(wt, mt, g0)
                        ob = stg[st["n"] % 2]
                        ACT(ob, ob[:, 0:TT], bk, bk[:, 0:TT], AF.Sigmoid)
                        STO(ob, GTs[(blk * 4 + mt) * 128:(blk * 4 + mt + 1) * 128, g0:g0 + TT], ob[:, 0:TT])

        def phase_conv(li):
            phase_reset()
            xc = [alloc([128, T], F32, "xc%d" % i) for i in range(2)]
            acc = [alloc([128, T], F32, "acc%d" % i) for i in range(2)]
            ys = [alloc([128, T], F32, "ys%d" % i) for i in range(2)]
            ysb = [alloc([128, T], BF16, "ysb%d" % i) for i in range(2)]
            tst = [alloc([128, 4, 128], F32, "tst%d" % i) for i in range(2)]
            tsb = [alloc([128, 4, 128], BF16, "tsb%d" % i) for i in range(2)]
            n = 0
            for ch in range(24):
                x_, a_, y_, yb_ = xc[ch % 2], acc[ch % 2], ys[ch % 2], ysb[ch % 2]
                LD(x_, x_[:], XBC[ch * 128:(ch + 1) * 128, :])
                for (sa, sb_) in ((0, L), (L, T)):
                    TS("dve", a_, a_[:, sa:sb_], x_, x_[:, sa:sb_], convw[:, ch, 2:3], convb[:, ch:ch + 1], ALU.mult, ALU.add, extra=[convw, convb])
                    for j in (0, 1, 3, 4):
                        o = j - 2
                        ta, tb = max(sa, sa - o), min(sb_, sb_ - o)
                        STT("dve", a_, a_[:, ta:tb], x_, x_[:, ta + o:tb + o], convw[:, ch, j:j + 1], a_, a_[:, ta:tb], ALU.mult, ALU.add, extra=[convw])
                ACT(y_, y_[:], a_, a_[:], AF.Silu)
                if ch >= 16:
                    CP("pool", yb_, yb_[:], y_, y_[:])
                    g = (ch - 16) % 4
                    STO(yb_, (BTs if ch < 20 else CTs)[g, :, :], yb_[:])
                if ch < 20:
                    for c0 in range(0, NCH, 4):
                        ncg = min(4, NCH - c0)
                        bk = banks[n % 2]
                        for cc in range(ncg):
                            c = c0 + cc
                            TR(bk, bk[:, cc * 128:(cc + 1) * 128], y_, y_[:, c * 128:(c + 1) * 128], ident, ident[:])
                        if ch < 16:
                            tb_ = tst[n % 2]
                            ACT(tb_, tb_[:, 0:ncg, :], bk, bk[:, 0:ncg * 128].rearrange("p (c f) -> p c f", f=128), AF.Copy)
                            STO(tb_, XStok.rearrange("(c p) f -> p c f", p=128)[:, c0:c0 + ncg, ch * 128:(ch + 1) * 128], tb_[:, 0:ncg, :])
                        else:
                            tb_ = tsb[n % 2]
                            ACT(tb_, tb_[:, 0:ncg, :], bk, bk[:, 0:ncg * 128].rearrange("p (c f) -> p c f", f=128), AF.Copy)
                            STO(tb_, Btok.rearrange("(c p) f -> p c f", p=128)[:, c0:c0 + ncg, (ch - 16) * 128:(ch - 15) * 128], tb_[:, 0:ncg, :])
                        n += 1

        def phase_att(li):
            phase_reset()
            kt = [alloc([128, T], BF16, "kt%d" % i) for i in range(2)]
            qt = [alloc([128, T], BF16, "qt%d" % i) for i in range(2)]
            vt = [alloc([128, NCH, 128], BF16, "vt%d" % i) for i in range(2)]
            pe_ = [alloc([128, QT_], BF16, "p%d" % i) for i in range(4)]
            rr = [alloc([128, QT_], F32, "rr%d" % i) for i in range(2)]
            o0 = alloc([128, QT_], F32, "o0")
            o1 = alloc([128, QT_], F32, "o1")
            osq = alloc([128, QT_], BF16, "osq")
            ob = [alloc([128, QT_], BF16, "ob%d" % i) for i in range(2)]
            n = 0
            for hd in range(HEADS):
                k_, q_, v_ = kt[hd % 2], qt[hd % 2], vt[hd % 2]
                LD(k_, k_[:], KTs[hd, :, :])
                LD(q_, q_[:], QTs[hd, :, :])
                LD(v_, v_[:], Vtok.rearrange("(c p) f -> p c f", p=128)[:, :, hd * 128:(hd + 1) * 128])
                qtiles = [(q0, q0 + QT_, list(range(NCH))) for q0 in range(0, L, QT_)]
                qtiles += [(L, T, list(range(L // 128, NCH)))]
                for (qa, qb, kcs) in qtiles:
                    nq = qb - qa
                    num0, num1, den0, den1 = banks[4], banks[5], banks[6], banks[7]
                    def scores(kc):
                        nonlocal n
                        ks = slice(kc * 128, (kc + 1) * 128)
                        s0, s1 = banks[(n % 2) * 2], banks[(n % 2) * 2 + 1]
                        p0, p1 = pe_[(n % 2) * 2], pe_[(n % 2) * 2 + 1]
                        n += 1
                        MM(s0, s0[:, 0:nq], k_, k_[0:64, ks], q_, q_[0:64, qa:qb], True, True)
                        MM(s1, s1[:, 0:nq], k_, k_[64:128, ks], q_, q_[64:128, qa:qb], True, True)
                        ACT(p0, p0[:, 0:nq], s0, s0[:, 0:nq], AF.Exp, scale=0.125)
                        ACT(p1, p1[:, 0:nq], s1, s1[:, 0:nq], AF.Exp, scale=0.125)
                        return p0, p1

                    pend = scores(kcs[0])
                    for ki, kc in enumerate(kcs):
                        first, last = ki == 0, ki == len(kcs) - 1
                        p0, p1 = pend
                        if not last:
                            pend = scores(kcs[ki + 1])
                        MM(num0, num0[:, 0:nq], v_, v_[:, kc, :], p0, p0[:, 0:nq], first, last)
                        MM(den0, den0[:, 0:nq], onesb, onesb[:], p0, p0[:, 0:nq], first, last)
                        MM(num1, num1[:, 0:nq], v_, v_[:, kc, :], p1, p1[:, 0:nq], first, last)
                        MM(den1, den1[:, 0:nq], onesb, onesb[:], p1, p1[:, 0:nq], first, last)
                    RECIP(rr[0], rr[0][:, 0:nq], den0, den0[:, 0:nq])
                    RECIP(rr[1], rr[1][:, 0:nq], den1, den1[:, 0:nq])
                    TTo("dve", o0, o0[:, 0:nq], num0, num0[:, 0:nq], rr[0], rr[0][:, 0:nq], ALU.mult)
                    TTo("dve", o1, o1[:, 0:nq], num1, num1[:, 0:nq], rr[1], rr[1][:, 0:nq], ALU.mult)
                    STT("dve", o0, o0[:, 0:nq], o1, o1[:, 0:nq], lamw[:, 5:6], o0, o0[:, 0:nq], ALU.mult, ALU.add, extra=[lamw])
                    ACT(osq, osq[:, 0:nq], o0, o0[:, 0:nq], AF.Square)
                    MM(den0, den0[:, 0:nq], onesb, onesb[:], osq, osq[:, 0:nq], True, True)
                    ACT(rr[0], rr[0][:, 0:nq], den0, den0[:, 0:nq], AF.Sqrt, bias=EPS, scale=1.0 / 128)
                    RECIP(rr[1], rr[1][:, 0:nq], rr[0], rr[0][:, 0:nq])
                    o_ = ob[n % 2]
                    STT("dve", o_, o_[:, 0:nq], o0, o0[:, 0:nq], subw[:, 0:1], rr[1], rr[1][:, 0:nq], ALU.mult, ALU.mult, extra=[subw])
                    STO(o_, OATT[hd * 128:(hd + 1) * 128, qa:qb], o_[:, 0:nq])

        def phase_ssm(li):
            w = W[li]
            phase_reset()
            nlat = L // 128
            fwd = list(range(nlat, NCH)) + list(range(nlat))
            bwd = list(range(NCH - 1, nlat - 1, -1)) + list(range(nlat - 1, -1, -1))
            hsave = alloc([128, NCH, D], BF16, "hsave")
            hf = alloc([128, D], F32, "hf")
            hb = alloc([128, D], F32, "hb")
            hbb = alloc([128, D], BF16, "hbb")
            dskip = alloc([128, D], F32, "dskip")
            ssmn = alloc([128, D], F32, "ssmn")
            LD(dskip, dskip[:], w["dskip"][:, :])
            LD(ssmn, ssmn[:], w["ssmn"][:, :])
            xs = [alloc([128, D], F32, "xs%d" % i) for i in range(1)] * 2
            dtda = [alloc([128, 128], F32, "dtda%d" % i) for i in range(2)]
            btk = [alloc([128, 512], BF16, "btk%d" % i) for i in range(2)]
            btf = [alloc([128, 4, 128], BF16, "btf%d" % i) for i in range(2)]
            ctf = [alloc([128, 4, 128], BF16, "ctf%d" % i) for i in range(2)]
            sz = [alloc([128, D], F32, "sz%d" % i) for i in range(1)] * 2
            wc = alloc([128, 64], F32, "wc")
            ea = alloc([128, 64], F32, "ea")
            dw = alloc([128, 64], F32, "dw")
            xdw = alloc([128, D], BF16, "xdw")
            xdt = [alloc([128, D], BF16, "xdt%d" % i) for i in range(2)]
            cbm = [alloc([128, 4, 128], F32, "cbm%d" % i) for i in range(2)]
            LT = [alloc([128, 4, 128], F32, "LT%d" % i) for i in range(2)]
            RB = [alloc([128, 4, 128], F32, "RB%d" % i) for i in range(2)]
            E = [alloc([128, 4, 128], F32, "E%d" % i) for i in range(2)]
            EA = [alloc([128, 4, 128], F32, "EA%d" % i) for i in range(2)]
            MT = [alloc([128, 4, 128], BF16, "MT%d" % i) for i in range(2)]
            CE = [alloc([128, 4, 128], BF16, "CE%d" % i) for i in range(2)]
            yt = alloc([128, 512], F32, "yt")
            gz = alloc([128, 512], F32, "gz")
            junk = alloc([128, 512], F32, "junk")
            ssq = alloc([128, 4], F32, "ssq")
            nacol = alloc([128, 64], F32, "nacol")
            ob = [alloc([128, 512], BF16, "ob%d" % i) for i in range(2)]
            ot = [alloc([128, 4, 128], BF16, "ot%d" % i) for i in range(2)]
            identb = alloc([128, 128], BF16, "identb")
            CP("dve", identb, identb[:], ident, ident[:])
            P.op("dve", lambda e: e.memset(hf[:], 0.0), [], [hf])
            P.op("dve", lambda e: e.memset(hb[:], 0.0), [], [hb])
            P.op("pool", lambda e: e.memset(hbb[:], 0.0), [], [hbb])
            MU = {0: (0, 1), 1: (2, 3)}
            cbank, sbank = banks[0], banks[1]

            def tokv(Xd, c):
                return Xd[c * 128:(c + 1) * 128, :]

            def state_update(d, c_i, x_, dd_, bt_, hst):
                U = masks[:, MU[d][0], :]
                da = dd_[:, 64 + d * 32:64 + (d + 1) * 32]
                MM(cbank, cbank[:, 0:32], masks, U, dd_, da, True, True)
                MM(cbank, cbank[:, 32:64], onesf, onesf[:], dd_, da, True, True)
                ACT(wc, wc[:, 0:32], cbank, cbank[:, 0:32], AF.Exp)
                ACT(ea, ea[:, 0:32], cbank, cbank[:, 32:64], AF.Exp)
                TTo("dve", dw, dw[:, 0:32], wc, wc[:, 0:32], dd_, dd_[:, d * 32:(d + 1) * 32], ALU.mult)
                TTo("dve", xdw, xdw[:].rearrange("p (r q) -> p r q", q=64), x_, x_[:].rearrange("p (r q) -> p r q", q=64),
                    dw, dw[:, 0:32].unsqueeze(2).to_broadcast([128, 32, 64]), ALU.mult)
                for g in range(4):
                    MM(sbank, sbank[:], bt_, bt_[:, g * 128:(g + 1) * 128], xdw, xdw[:, g * 512:(g + 1) * 512], True, True)
                    hv = hst[:, g * 512:(g + 1) * 512].rearrange("p (r q) -> p r q", q=64)
                    TTo("dve", hst, hv, hst, hv, ea, ea[:, g * 8:(g + 1) * 8].unsqueeze(2).to_broadcast([128, 8, 64]), ALU.mult)
                    TTo("dve", hst, hst[:, g * 512:(g + 1) * 512], hst, hst[:, g * 512:(g + 1) * 512], sbank, sbank[:], ALU.add)

            for i, c in enumerate(fwd):
                x_, dd_, bt_ = xs[i % 2], dtda[i % 2], btk[i % 2]
                LD(x_, x_[:], tokv(XStok, c))
                LD(dd_, dd_[:], tokv(DTDA, c))
                LD(bt_, bt_[:], tokv(Btok, c))
                ACT(hsave, hsave[:, c, :], hf, hf[:], AF.Copy)
                if i < NCH - 1:
                    state_update(0, c, x_, dd_, bt_, hf)
            n = 0
            for i, c in enumerate(bwd):
                x_, dd_, bt_, sz_ = xs[i % 2], dtda[i % 2], btk[i % 2], sz[i % 2]
                bf_, cf_ = btf[i % 2], ctf[i % 2]
                LD(x_, x_[:], tokv(XStok, c))
                LD(dd_, dd_[:], tokv(DTDA, c))
                LD(bt_, bt_[:], tokv(Btok, c))
                LD(sz_, sz_[:], tokv(SZ, c))
                LD(bf_, bf_[:], BTs[:, :, c * 128:(c + 1) * 128].rearrange("g p t -> p g t"))
                LD(cf_, cf_[:], CTs[:, :, c * 128:(c + 1) * 128].rearrange("g p t -> p g t"))
                for g in range(4):
                    MM(cbank, cbank[:, g * 128:(g + 1) * 128], bf_, bf_[:, g, :], cf_, cf_[:, g, :], True, True)
                for d in range(2):
                    TTo("dve", cbm[d], cbm[d][:], cbank, cbank[:].rearrange("p (g l) -> p g l", l=128), masks,
                        masks[:, 4 + d, :].unsqueeze(1).to_broadcast([128, 4, 128]), ALU.mult)
                    TTo("pool", xdt[d], xdt[d][:].rearrange("p (r q) -> p r q", q=64), x_, x_[:].rearrange("p (r q) -> p r q", q=64),
                        dd_, dd_[:, d * 32:(d + 1) * 32].unsqueeze(2).to_broadcast([128, 32, 64]), ALU.mult)
                for d in range(2):
                    MM(sbank, sbank[:, d * 32:(d + 1) * 32], masks, masks[:, MU[d][1], :], dd_, dd_[:, 64 + d * 32:64 + (d + 1) * 32], True, True)
                ACT(nacol, nacol[:], sbank, sbank[:, 0:64], AF.Identity, scale=-1.0)
                units = [(g, d, hb_) for g in range(4) for d in range(2) for hb_ in range(2)]
                ycnt = {}

                def front(ui, k):
                    g, d, hb_ = units[ui]
                    rb_, bcb = RB[k % 2], banks[4 + k % 2]
                    h0 = g * 8 + hb_ * 4
                    tri = masks[:, MU[d][1], :]
                    dac = dd_[:, 64 + d * 32 + h0:64 + d * 32 + h0 + 4].unsqueeze(2).to_broadcast([128, 4, 128])
                    TTo("pool", rb_, rb_[:], masks, tri.unsqueeze(1).to_broadcast([128, 4, 128]), dd_, dac, ALU.mult)
                    MM(bcb, bcb[:], onesf, onesf[:], rb_, rb_[:].rearrange("p r l -> p (r l)"), True, True)

                def back(ui, k):
                    g, d, hb_ = units[ui]
                    e_, ea_, mt_, ce_, bcb = E[k % 2], EA[k % 2], MT[k % 2], CE[k % 2], banks[4 + k % 2]
                    ybank = banks[6 + g % 2]
                    h0 = g * 8 + hb_ * 4
                    ACT(ea_, ea_[:].rearrange("p r l -> p (r l)"), bcb, bcb[:], AF.Exp)
                    for r4 in range(4):
                        ACT(e_, e_[:, r4, :], bcb, bcb[:, r4 * 128:(r4 + 1) * 128], AF.Exp,
                            bias=nacol[:, d * 32 + h0 + r4:d * 32 + h0 + r4 + 1], extra=[nacol])
                    STT("dve", mt_, mt_[:], e_, e_[:], 1.0, cbm[d], cbm[d][:, g, :].unsqueeze(1).to_broadcast([128, 4, 128]), ALU.min, ALU.mult)
                    TTo("pool", ce_, ce_[:], ea_, ea_[:], cf_, cf_[:, g, :].unsqueeze(1).to_broadcast([128, 4, 128]), ALU.mult)
                    for r4 in range(4):
                        hh = h0 + r4
                        r = hb_ * 4 + r4
                        ys_ = ybank[:, r * 64:(r + 1) * 64]
                        hs_b, hs_ap = (hsave, hsave[:, c, hh * 64:(hh + 1) * 64]) if d == 0 else (hbb, hbb[:, hh * 64:(hh + 1) * 64])
                        cn = ycnt.get(g, 0)
                        MM(ybank, ys_, mt_, mt_[:, r4, :], xdt[d], xdt[d][:, hh * 64:(hh + 1) * 64], cn == 0, False)
                        MM(ybank, ys_, ce_, ce_[:, r4, :], hs_b, hs_ap, False, cn == 30)
                        ycnt[g] = cn + 2

                def evac(g):
                        ybank = banks[6 + g % 2]
                        gs = slice(g * 512, (g + 1) * 512)
                        TTo("pool", yt, yt[:], x_, x_[:, gs], dskip, dskip[:, gs], ALU.mult)
                        TTo("dve", yt, yt[:], yt, yt[:], ybank, ybank[:], ALU.add)
                        TTo("dve", gz, gz[:], yt, yt[:], sz_, sz_[:, gs], ALU.mult)
                        P.op("dve", lambda e: e.memset(ssq[:, 0:1], 0.0), [], [ssq])
                        ACT(junk, junk[:], gz, gz[:], AF.Square, accum=ssq[:, 0:1], accb=ssq)
                        ACT(ssq, ssq[:, 1:2], ssq, ssq[:, 0:1], AF.Sqrt, bias=EPS, scale=1.0 / 512)
                        RECIP(ssq, ssq[:, 2:3], ssq, ssq[:, 1:2])
                        o_ = ob[g % 2]
                        STT("dve", o_, o_[:], gz, gz[:], ssq[:, 2:3], ssmn, ssmn[:, gs], ALU.mult, ALU.mult, extra=[ssq])
                        tbk = banks[0] if False else banks[1]
                        tv = tbk[:].bitcast(BF16)
                        for q in range(4):
                            TR(tbk, tv[:, q * 128:(q + 1) * 128], o_, o_[:, q * 128:(q + 1) * 128], identb, identb[:])
                        ot_ = ot[g % 2]
                        CP("dve", ot_, ot_[:], tbk, tv[:, 0:512].rearrange("p (q l) -> p q l", l=128))
                        STO(ot_, OSSM[g * 512:(g + 1) * 512, c * 128:(c + 1) * 128].rearrange("(q p) l -> p q l", p=128), ot_[:])

                front(0, n)
                for ui in range(16):
                    if ui + 1 < 16:
                        front(ui + 1, n + ui + 1)
                    back(ui, n + ui)
                    if ui % 4 == 3:
                        evac(units[ui][0])
                n += 16
                if i < NCH - 1:
                    state_update(1, c, x_, dd_, bt_, hb)
                    CP("pool", hbb, hbb[:], hb, hb[:])

        def phase_merge(li, Xin, Xout):
            w = W[li]
            phase_reset()
            MPASS = min(1152, T)
            assert T % MPASS == 0
            npt = MPASS // TT
            oa = alloc([128, 8, MPASS], BF16, "oa")
            os_ = alloc([128, NK, MPASS], BF16, "os")
            mm = alloc([128, NK, MPASS], BF16, "mm")
            wa = [alloc([128, 8, 128], BF16, "wa%d" % i) for i in range(2)]
            ws = [alloc([128, NK, 128], BF16, "ws%d" % i) for i in range(2)]
            wo = [alloc([128, NK, 128], BF16, "wo%d" % i) for i in range(2)]
            ga = [alloc([128, TT], F32, "ga%d" % i) for i in range(2)]
            gs_ = [alloc([128, TT], F32, "gs%d" % i) for i in range(2)]
            t1 = [alloc([128, TT], F32, "t1%d" % i) for i in range(2)]
            t2 = [alloc([128, TT], F32, "t2%d" % i) for i in range(2)]
            xm = [alloc([128, TT], F32, "xm%d" % i) for i in range(2)]
            xo = [alloc([128, TT], F32, "xo%d" % i) for i in range(2)]
            cnt = 0
            wst = WS()
            for p in range(T // MPASS):
                for m in range(NK):
                    wst.add(wa[m % 2], wa[m % 2][:], w["w_ba"][m], 8)
                    wst.add(ws[m % 2], ws[m % 2][:], w["w_bs"][m], NK)
                    wst.mark()
                for m in range(NK):
                    wst.add(wo[m % 2], wo[m % 2][:], w["w_out"][m], NK)
                    wst.mark()
            ui = 0
            for p in range(T // MPASS):
                t0 = p * MPASS
                LD(oa, oa[:], OATT.rearrange("(k p) t -> p k t", p=128)[:, :, t0:t0 + MPASS])
                LD(os_, os_[:], OSSM.rearrange("(k p) t -> p k t", p=128)[:, :, t0:t0 + MPASS])
                for m in range(NK):
                    wa_, ws_ = wa[m % 2], ws[m % 2]
                    wst.unit(ui)
                    ui += 1
                    for tt in range(npt):
                        g0 = t0 + tt * TT
                        cs = slice(tt * TT, (tt + 1) * TT)
                        pa, ps_ = banks[(cnt % 2) * 2], banks[(cnt % 2) * 2 + 1]
                        ga_, gs2, a1, a2 = ga[cnt % 2], gs_[cnt % 2], t1[cnt % 2], t2[cnt % 2]
                        cnt += 1
                        LD(ga_, ga_[:], GTs[m * 128:(m + 1) * 128, g0:g0 + TT])
                        LD(gs2, gs2[:], GTs[D + m * 128:D + (m + 1) * 128, g0:g0 + TT])
                        for k in range(8):
                            MM(pa, pa[:, 0:TT], wa_, wa_[:, k, :], oa, oa[:, k, cs], k == 0, k == 7)
                        for k in range(NK):
                            MM(ps_, ps_[:, 0:TT], ws_, ws_[:, k, :], os_, os_[:, k, cs], k == 0, k == NK - 1)
                        TTo("dve", a1, a1[:], pa, pa[:, 0:TT], ga_, ga_[:], ALU.mult)
                        TTo("dve", a2, a2[:], ps_, ps_[:, 0:TT], gs2, gs2[:], ALU.mult)
                        TTo("pool", mm, mm[:, m, cs], a1, a1[:], a2, a2[:], ALU.add)
                for m in range(NK):
                    wo_ = wo[m % 2]
                    wst.unit(ui)
                    ui += 1
                    for tt in range(npt):
                        g0 = t0 + tt * TT
                        cs = slice(tt * TT, (tt + 1) * TT)
                        po = banks[4 + cnt % 2]
                        xmt, xot = xm[cnt % 2], xo[cnt % 2]
                        cnt += 1
                        LD(xmt, xmt[:], Xin[m * 128:(m + 1) * 128, g0:g0 + TT])
                        for k in range(NK):
                            MM(po, po[:, 0:TT], wo_, wo_[:, k, :], mm, mm[:, k, cs], k == 0, k == NK - 1)
                        for (a, b, i) in segs(g0, g0 + TT):
                            STT("dve", xot, xot[:, a - g0:b - g0], po, po[:, a - g0:b - g0], GTg[1][:, m, i:i + 1],
                                xmt, xmt[:, a - g0:b - g0], ALU.mult, ALU.add, extra=[GTg[1]])
                        STO(xot, Xout[m * 128:(m + 1) * 128, g0:g0 + TT], xot[:])

        cur = xT
        for li in range(nlayers):
            last = li == nlayers - 1
            phase_prep(li)
            phase_ffn(li, 0, cur, XA if cur is not XA else XB)
            cur = XA if cur is not XA else XB
            phase_proj(li, cur)
            phase_conv(li)
            phase_att(li)
            phase_ssm(li)
            nxt = XB if cur is XA else XA
            phase_merge(li, cur, nxt)
            cur = nxt
            nxt = XB if cur is XA else XA
            phase_ffn(li, 1, cur, nxt, final=last)
            cur = nxt
        bfs = dict(QTs=QTs, KTs=KTs, Vtok=Vtok, Btok=Btok, BTs=BTs, CTs=CTs, OATT=OATT, OSSM=OSSM)
        for name in debug:
            if name in bfs:
                src = bfs[name]
                if len(src.shape) == 3:
                    src = src.rearrange("h p t -> (h p) t")
                rows, cols = src.shape
                dst = nc.dram_tensor(name + "_f", [rows, cols], F32, kind="ExternalOutput").ap()
                phase_reset()
                tb = alloc([128, cols], BF16, "dbgb")
                tf = alloc([128, cols], F32, "dbgf")
                for r0 in range(0, rows, 128):
                    LD(tb, tb[:], src[r0:r0 + 128, :])
                    CP("dve", tf, tf[:], tb, tb[:])
                    STO(tf, dst[r0:r0 + 128, :], tf[:])
        P.barrier()
        P.finalize()
    return nc


def _blk(wm):
    K, M = wm.shape
    return np.ascontiguousarray(wm.reshape(K // 128, 128, M // 128, 128).transpose(2, 1, 0, 3)).reshape(M // 128, 128, (K // 128) * 128)


def _col(v):
    n = v.shape[0] // 128
    return np.ascontiguousarray(v.reshape(n, 128).T)


def _rope_tables(L):
    n_freq = 16
    inv = (10000.0 ** (-np.arange(n_freq, dtype=np.float32) / n_freq)).astype(np.float32)
    rows = L // 64
    row = np.repeat(np.arange(rows, dtype=np.float32), 64)
    col = np.tile(np.arange(64, dtype=np.float32), rows)
    ang = np.concatenate([row[:, None] * inv, col[:, None] * inv], axis=-1).astype(np.float32)
    cos, sin = np.cos(ang).astype(np.float32), np.sin(ang).astype(np.float32)
    idx = (np.arange(128) % 64) // 2
    return np.ascontiguousarray(cos[:, idx].T), np.ascontiguousarray(sin[:, idx].T)


def _constants(L):
    j = np.arange(128)[:, None]
    s = np.arange(128)[None, :]
    masks = np.stack([(j > s), (j <= s), (j < s), (j >= s), (s >= j), (s <= j)], axis=1).astype(np.float32)
    blockones = ((j // 64) == (s // 64)).astype(np.float32)
    rmat = np.zeros((128, 128), np.float32)
    for i in range(64):
        rmat[2 * i + 1, 2 * i] = -1.0
        rmat[2 * i, 2 * i + 1] = 1.0
    cosT, sinT = _rope_tables(L)
    return dict(ident=np.eye(128, dtype=np.float32), masks=np.ascontiguousarray(masks), blockones=blockones,
                rmat=rmat, cosT=cosT, sinT=sinT)


def prep_shared(inp, L, nlayers=2):
    f = lambda a: np.asarray(a, dtype=np.float32)
    sh = _constants(L)
    for li in range(nlayers):
        s = str(li)
        sh["ada_w" + s] = f(inp["ada_w"][li])
        sh["ada_bT" + s] = _col(f(inp["ada_b"][li]))
        sh["normT" + s] = np.ascontiguousarray(np.stack([_col(f(inp[k][li])) for k in ("ffn1_norm", "mix_norm", "ffn2_norm")], axis=1))
        for n_, (kg, kd) in enumerate((("ffn1_w_gu", "ffn1_w_down"), ("ffn2_w_gu", "ffn2_w_down"))):
            wgu = f(inp[kg][li])
            a = wgu.reshape(NK, 128, 2, NJ, 128).transpose(3, 1, 2, 0, 4)
            sh["wgu%d_%s" % (n_ + 1, s)] = np.ascontiguousarray(a).reshape(NJ, 128, 2 * NK * 128)
            sh["wd%d_%s" % (n_ + 1, s)] = _blk(f(inp[kd][li]))
        sh["w_in" + s] = f(inp["w_in"][li])
        qn, kn = f(inp["q_norm"][li]), f(inp["k_norm"][li])
        sh["qkn" + s] = np.ascontiguousarray(np.stack([np.tile(qn, 2), np.tile(kn, 2)], axis=1))
        lv = np.stack([f(inp[k][li]) for k in ("lambda_q1", "lambda_k1", "lambda_q2", "lambda_k2")], axis=0)
        sh["lamv" + s] = np.ascontiguousarray(np.broadcast_to(lv[None], (128, 4, 64)))
        sh["subln" + s] = f(inp["attn_subln"][li]).reshape(128, 1).copy()
        cw = f(inp["conv_w"][li])
        sh["convw" + s] = np.ascontiguousarray(cw.reshape(5, 24, 128).transpose(2, 1, 0))
        sh["convb" + s] = _col(f(inp["conv_b"][li]))
        sh["dtb" + s] = np.ascontiguousarray(np.broadcast_to(f(inp["dt_bias"][li]).reshape(1, 64), (128, 64)))
        sh["alog" + s] = np.ascontiguousarray(np.broadcast_to(f(inp["a_log"][li]).reshape(1, 64), (128, 64)))
        sh["dskip" + s] = np.ascontiguousarray(np.broadcast_to(np.repeat(f(inp["d_skip"][li]), 64)[None], (128, D)))
        sh["ssmn" + s] = np.ascontiguousarray(np.broadcast_to(f(inp["ssm_norm"][li])[None], (128, D)))
        sh["w_ba" + s] = _blk(f(inp["w_branch_attn"][li]))
        sh["w_bs" + s] = _blk(f(inp["w_branch_ssm"][li]))
        sh["w_out" + s] = _blk(f(inp["w_out"][li]))
    return sh


def prep_core(inp, b):
    x, ctx = np.asarray(inp["x"][b], np.float32), np.asarray(inp["ctx"][b], np.float32)
    xT = np.ascontiguousarray(np.concatenate([x, ctx], axis=0).T)
    c = np.asarray(inp["c"][b], np.float32)
    cc = np.asarray(inp["c_ctx"], np.float32)
    condT = np.ascontiguousarray(np.stack([_col(c), _col(cc)], axis=2))
    return dict(xT=xT, condT=condT)


def kernel(**inputs):
    B, L, _ = inputs["x"].shape
    LC = inputs["ctx"].shape[1]
    nc = build(L, LC, 2)
    sh = prep_shared(inputs, L, 2)
    in_maps = []
    for b in range(B):
        m = dict(sh)
        m.update(prep_core(inputs, b))
        in_maps.append(m)
    res = run_bass_kernel_spmd(nc, in_maps, core_ids=list(range(B)))
    out = np.stack([np.ascontiguousarray(r["yT"].T) for r in res.results], axis=0)
    return out.astype(np.float32)
```

```python
import contextlib
import math
import numpy as np
import concourse.bass as bass
import concourse.mybir as mybir
from concourse.bass_utils import run_bass_kernel_spmd

F32 = mybir.dt.float32
BF16 = mybir.dt.bfloat16
AF = mybir.ActivationFunctionType
ALU = mybir.AluOpType
AX = mybir.AxisListType

D = 2048
DFF = 5632
NJ = DFF // 128
NK = D // 128
HEADS = 8
EPS = 1e-6
IN_COLS = 12352
ENGINES = ("pe", "act", "dve", "pool", "sp")


class Buf:
    __slots__ = ("name", "last_w", "readers", "t", "key")

    def __init__(self, name, t=None, key=None):
        self.name = name
        self.last_w = None
        self.readers = []
        self.t = t
        self.key = key

    def __getitem__(self, idx):
        return self.t[idx]


class Op:
    __slots__ = ("eng", "fn", "deps", "dma", "key", "sig", "signal", "idx")


class Prog:
    def __init__(self, nc):
        self.nc = nc
        self.ops = []
        self.last_eng = {}
        self.last_dma = {}

    def op(self, eng, fn, reads=(), writes=(), dma=False, key=None):
        o = Op()
        o.eng, o.fn, o.dma, o.key = eng, fn, dma, key
        o.signal, o.sig, o.idx = False, None, len(self.ops)
        deps = set()
        for r in reads:
            if r.last_w is not None:
                deps.add(r.last_w)
        for w in writes:
            if w.last_w is not None:
                deps.add(w.last_w)
            lastr = {}
            for rd in w.readers:
                ro = self.ops[rd]
                lastr[("dma", ro.key) if ro.dma else ro.eng] = rd
            deps.update(lastr.values())
        o.deps = deps
        for r in reads:
            r.readers.append(o.idx)
        for w in writes:
            w.last_w = o.idx
            w.readers = []
        self.ops.append(o)
        self.last_eng[eng] = o.idx
        if dma:
            self.last_dma[key] = o.idx
        return o

    def barrier(self):
        deps = set(self.last_eng.values()) | set(self.last_dma.values())
        for e in ENGINES:
            o = Op()
            o.eng, o.fn, o.dma, o.key = e, None, False, None
            o.signal, o.sig, o.idx = False, None, len(self.ops)
            o.deps = set(deps)
            self.ops.append(o)
        self.last_dma = {}

    def finalize(self):
        nc, ops = self.nc, self.ops
        for o in ops:
            for d in o.deps:
                od = ops[d]
                if od.dma:
                    continue
                if od.eng == o.eng and od.eng == "pe" and not o.dma:
                    continue
                od.signal = True
        sem_names, counts = {}, {}
        for o in ops:
            if o.fn is None:
                continue
            if o.dma:
                k = ("dma", o.key)
                sem_names.setdefault(k, "d_%s" % o.key)
                counts[k] = counts.get(k, 0) + 16
                o.sig = (k, counts[k])
            elif o.signal:
                k = ("eng", o.eng)
                sem_names.setdefault(k, "e_" + o.eng)
                counts[k] = counts.get(k, 0) + 1
                o.sig = (k, counts[k])
        with contextlib.ExitStack() as st:
            sems = {k: st.enter_context(nc.semaphore(n)) for k, n in sem_names.items()}
            block = st.enter_context(nc.Block())
            per_eng = {e: [] for e in ENGINES}
            for o in ops:
                per_eng[o.eng].append(o)

            def make(engname):
                def body(eng):
                    known = {}
                    for o in per_eng[engname]:
                        waits = {}
                        for d in o.deps:
                            od = ops[d]
                            if od.sig is None:
                                continue
                            k, v = od.sig
                            if known.get(k, 0) >= v:
                                continue
                            if waits.get(k, 0) < v:
                                waits[k] = v
                        for k, v in waits.items():
                            eng.wait_ge(sems[k], v)
                            known[k] = v
                        if o.fn is None:
                            continue
                        ins = o.fn(eng)
                        if o.sig is not None:
                            ins.then_inc(sems[o.sig[0]], 16 if o.dma else 1)
                return body

            block.tensor(make("pe"))
            block.scalar(make("act"))
            block.vector(make("dve"))
            block.gpsimd(make("pool"))
            block.sync(make("sp"))


def build(L, LC, nlayers=2, debug=()):
    T = L + LC
    TT = 384
    assert T % TT == 0 and L % 128 == 0 and LC % 128 == 0
    PASS = min(768, T)
    assert T % PASS == 0
    NCH = T // 128
    QT_ = min(512, L)
    nc = bass.Bass("TRN2", target_bir_lowering=False)

    def din(name, shape, dt=F32):
        return nc.dram_tensor(name, list(shape), dt, kind="ExternalInput").ap()

    def dscr(name, shape, dt=F32):
        kind = "ExternalOutput" if (name in debug and dt == F32) else "Internal"
        return nc.dram_tensor(name, list(shape), dt, kind=kind).ap()

    xT = din("xT", [D, T])
    condT = din("condT", [128, NK, 2])
    ident_d = din("ident", [128, 128])
    masks_d = din("masks", [128, 6, 128])
    blk1_d = din("blockones", [128, 128])
    rmat_d = din("rmat", [128, 128])
    cos_d = din("cosT", [128, L])
    sin_d = din("sinT", [128, L])
    W = []
    for li in range(nlayers):
        s = str(li)
        W.append(dict(
            ada_w=din("ada_w" + s, [D, 9 * D]), ada_bT=din("ada_bT" + s, [128, 144]),
            normT=din("normT" + s, [128, 3, NK]),
            wgu=[din("wgu1_" + s, [NJ, 128, 2 * NK * 128]), din("wgu2_" + s, [NJ, 128, 2 * NK * 128])],
            wd=[din("wd1_" + s, [NK, 128, NJ * 128]), din("wd2_" + s, [NK, 128, NJ * 128])],
            w_in=din("w_in" + s, [D, IN_COLS]),
            qkn=din("qkn" + s, [128, 2]), lamv=din("lamv" + s, [128, 4, 64]), subln=din("subln" + s, [128, 1]),
            convw=din("convw" + s, [128, 24, 5]), convb=din("convb" + s, [128, 24]),
            dtb=din("dtb" + s, [128, 64]), alog=din("alog" + s, [128, 64]),
            dskip=din("dskip" + s, [128, D]), ssmn=din("ssmn" + s, [128, D]),
            w_ba=din("w_ba" + s, [NK, 128, 8 * 128]), w_bs=din("w_bs" + s, [NK, 128, NK * 128]),
            w_out=din("w_out" + s, [NK, 128, NK * 128]),
        ))
    yT = nc.dram_tensor("yT", [D, L], F32, kind="ExternalOutput").ap()
    XA = dscr("XA", [D, T])
    XB = dscr("XB", [D, T])
    QTs = dscr("QTs", [HEADS, 128, T], BF16)
    KTs = dscr("KTs", [HEADS, 128, T], BF16)
    Vtok = dscr("Vtok", [T, 1024], BF16)
    SZ = dscr("SZ", [T, D])
    XBC = dscr("XBC", [3072, T])
    DTDA = dscr("DTDA", [T, 128])
    GTs = dscr("GTs", [2 * D, T])
    XStok = dscr("XStok", [T, D])
    Btok = dscr("Btok", [T, 512], BF16)
    BTs = dscr("BTs", [4, 128, T], BF16)
    CTs = dscr("CTs", [4, 128, T], BF16)
    OATT = dscr("OATT", [1024, T], BF16)
    OSSM = dscr("OSSM", [D, T], BF16)

    with contextlib.ExitStack() as st:
        NW = 52800
        arena = st.enter_context(nc.sbuf_tensor("arena", [128, NW], F32))
        banks = [Buf("bank%d" % i, st.enter_context(nc.psum_tensor("bank%d" % i, [128, 512], F32))) for i in range(8)]
        P = Prog(nc)
        A = {"off": 0, "base": 0, "slot": 0}

        def alloc(shape, dt=F32, name="t"):
            n = int(np.prod(shape[1:]))
            words = n if dt == F32 else (n + 1) // 2
            off = A["off"]
            assert off + words <= NW, ("arena overflow", name, off, words)
            A["off"] = off + words
            v = arena[:, off:off + words]
            if dt != F32:
                v = v.bitcast(dt)
            if len(shape) == 3:
                v = v.rearrange("p (a b) -> p a b", a=shape[1])
            elif len(shape) == 4:
                v = v.rearrange("p (a b c) -> p a b c", a=shape[1], b=shape[2])
            A["slot"] += 1
            return Buf(name, v, key="s%d" % A["slot"])

        def phase_reset():
            P.barrier()
            A["off"] = A["base"]
            A["slot"] = A["pslot"]

        def MM(outb, out, lb, lhsT, rb, rhs, start, stop):
            P.op("pe", lambda e: e.matmul(out, lhsT=lhsT, rhs=rhs, start=start, stop=stop), [lb, rb], [outb])

        def TR(outb, out, ib, in_, idb, idap):
            P.op("pe", lambda e: e.transpose(out=out, in_=in_, identity=idap), [ib, idb], [outb])

        def ACT(outb, out, ib, in_, func, bias=0.0, scale=1.0, extra=(), accum=None, accb=None):
            w = [outb] + ([accb] if accb is not None else [])
            if accum is None:
                P.op("act", lambda e: e.activation(out=out, in_=in_, func=func, bias=bias, scale=scale), [ib] + list(extra), w)
            else:
                P.op("act", lambda e: e.activation(out=out, in_=in_, func=func, bias=bias, scale=scale, accum_out=accum), [ib] + list(extra), w)

        def TTo(eng, outb, out, ab, a, bb, b, op):
            P.op(eng, lambda e: e.tensor_tensor(out=out, in0=a, in1=b, op=op), [ab, bb], [outb])

        def TS(eng, outb, out, ab, a, s1, s2, op0, op1=None, extra=()):
            if op1 is None:
                P.op(eng, lambda e: e.tensor_scalar(out=out, in0=a, scalar1=s1, scalar2=None, op0=op0), [ab] + list(extra), [outb])
            else:
                P.op(eng, lambda e: e.tensor_scalar(out=out, in0=a, scalar1=s1, scalar2=s2, op0=op0, op1=op1), [ab] + list(extra), [outb])

        def STT(eng, outb, out, ab, a, sc, bb, b, op0, op1, extra=()):
            P.op(eng, lambda e: e.scalar_tensor_tensor(out=out, in0=a, scalar=sc, in1=b, op0=op0, op1=op1), [ab, bb] + list(extra), [outb])

        def CP(eng, outb, out, ib, in_):
            P.op(eng, lambda e: e.tensor_copy(out=out, in_=in_), [ib], [outb])

        def RECIP(outb, out, ib, in_):
            P.op("dve", lambda e: e.reciprocal(out=out, in_=in_), [ib], [outb])

        def LD(buf, out, src, q="sp"):
            P.op(q, lambda e: e.dma_start(out=out, in_=src), [], [buf], dma=True, key=buf.key)

        def STO(buf, dst, src, q="sp"):
            P.op(q, lambda e: e.dma_start(out=dst, in_=src), [buf], [], dma=True, key=buf.key)

        RR = ("act", "dve", "pool")

        class WS:
            def __init__(self, nst=6, words=2048, la=4):
                self.st = [alloc([128, words], F32, "wst%d" % i) for i in range(nst)]
                self.ch, self.nd, self.nc_, self.nst, self.la, self.marks = [], 0, 0, nst, la, []

            def add(self, dstb, dst, src, k):
                self.ch.append((dstb, dst, src, k))

            def mark(self):
                self.marks.append(len(self.ch))

            def _view(self, i):
                dstb, dst, src, k = self.ch[i]
                s_ = self.st[i % self.nst]
                n = int(np.prod(src.shape[1:]))
                sv = s_[:, 0:n]
                if k is not None:
                    sv = sv.rearrange("p (k m) -> p k m", k=k)
                return s_, sv

            def upto(self, kk):
                kk = min(kk, len(self.ch))
                while self.nc_ < kk:
                    while self.nd < min(self.nc_ + 1 + self.la, len(self.ch)):
                        s_, sv = self._view(self.nd)
                        LD(s_, sv, self.ch[self.nd][2], q="act")
                        self.nd += 1
                    i = self.nc_
                    dstb, dst, src, k = self.ch[i]
                    s_, sv = self._view(i)
                    eng = RR[i % 3]
                    if eng == "act":
                        ACT(dstb, dst, s_, sv, AF.Copy)
                    else:
                        CP(eng, dstb, dst, s_, sv)
                    self.nc_ += 1

            def unit(self, ui):
                self.upto(self.marks[min(ui + 1, len(self.marks) - 1)])

        def segs(t0, t1):
            out = []
            if t0 < L:
                out.append((t0, min(t1, L), 0))
            if t1 > L:
                out.append((max(t0, L), t1, 1))
            return out

        ident = alloc([128, 128], F32, "ident")
        masks = alloc([128, 6, 128], F32, "masks")
        onesb = alloc([128, 128], BF16, "onesb")
        onesf = alloc([128, 128], F32, "onesf")
        blk1 = alloc([128, 128], BF16, "blk1")
        blk1f = alloc([128, 128], F32, "blk1f")
        rmat = alloc([128, 128], F32, "rmat")
        condb = alloc([128, NK, 2], BF16, "condb")
        condf = alloc([128, NK, 2], F32, "condf")
        modT = alloc([128, 144, 2], F32, "modT")
        Gs = alloc([128, 3, NK, 2], F32, "Gs") if False else None
        Gs = [alloc([128, NK, 2], F32, "G%d" % s) for s in range(3)]
        GTg = [alloc([128, NK, 2], F32, "GT%d" % s) for s in range(3)]
        normT = alloc([128, 3, NK], F32, "normT")
        adab = alloc([128, 144], F32, "adab")
        qkn = alloc([128, 2], F32, "qkn")
        lamv = alloc([128, 4, 64], F32, "lamv")
        lamw = alloc([128, 8], F32, "lamw")
        subw = alloc([128, 1], F32, "subw")
        convw = alloc([128, 24, 5], F32, "convw")
        convb = alloc([128, 24], F32, "convb")
        dtb = alloc([128, 64], F32, "dtb")
        aneg = alloc([128, 64], F32, "aneg")
        A["base"] = A["off"]
        A["pslot"] = A["slot"]

        LD(ident, ident[:], ident_d[:, :])
        LD(masks, masks[:], masks_d[:, :, :])
        LD(blk1f, blk1f[:], blk1_d[:, :])
        LD(rmat, rmat[:], rmat_d[:, :])
        LD(condf, condf[:], condT[:, :, :])
        P.op("dve", lambda e: e.memset(onesf[:], 1.0), [], [onesf])
        P.op("dve", lambda e: e.memset(onesb[:], 1.0), [], [onesb])
        CP("dve", blk1, blk1[:], blk1f, blk1f[:])
        ACT(condb, condb[:], condf, condf[:], AF.Silu)

        def modnorm(xt, slot, g0, ub, uout, sqb, rs, rsd, bank):
            ACT(sqb, sqb[:], xt, xt[:], AF.Square)
            for k in range(NK):
                MM(bank, bank[:, 0:TT], onesb, onesb[:], sqb, sqb[:, k, :], k == 0, k == NK - 1)
            ACT(rs, rs[:], bank, bank[:, 0:TT], AF.Sqrt, bias=EPS, scale=1.0 / D)
            RECIP(rsd, rsd[:], rs, rs[:])
            for (a, b, i) in segs(g0, g0 + TT):
                a -= g0
                b -= g0
                n = b - a
                gb = Gs[slot][:, :, i:i + 1].to_broadcast([128, NK, n])
                sb_ = modT[:, slot * 48:slot * 48 + NK, i:i + 1].to_broadcast([128, NK, n])
                rb = rsd[:, a:b].unsqueeze(1).to_broadcast([128, NK, n])
                TTo("pool", xt, xt[:, :, a:b], xt, xt[:, :, a:b], Gs[slot], gb, ALU.mult)
                TTo("dve", xt, xt[:, :, a:b], xt, xt[:, :, a:b], rsd, rb, ALU.mult)
                TTo("dve", ub, uout[:, :, a:b], xt, xt[:, :, a:b], modT, sb_, ALU.add)

        def xview(Xd):
            return Xd.rearrange("(k p) t -> p k t", p=128)

        def phase_prep(li):
            w = W[li]
            phase_reset()
            LD(adab, adab[:], w["ada_bT"][:, :])
            LD(normT, normT[:], w["normT"][:, :, :])
            LD(qkn, qkn[:], w["qkn"][:, :])
            LD(lamv, lamv[:], w["lamv"][:, :, :])
            LD(subw, subw[:], w["subln"][:, :])
            LD(convw, convw[:], w["convw"][:, :, :])
            LD(convb, convb[:], w["convb"][:, :])
            LD(dtb, dtb[:], w["dtb"][:, :])
            LD(aneg, aneg[:], w["alog"][:, :])
            ACT(aneg, aneg[:], aneg, aneg[:], AF.Exp)
            TS("dve", aneg, aneg[:], aneg, aneg[:], -1.0, None, ALU.mult)
            lam_init = 0.8 - 0.6 * math.exp(-0.3 * li)
            tmp = alloc([128, 2, 64], F32, "lamtmp")
            TTo("dve", tmp, tmp[:, 0, :], lamv, lamv[:, 0, :], lamv, lamv[:, 1, :], ALU.mult)
            TTo("dve", tmp, tmp[:, 1, :], lamv, lamv[:, 2, :], lamv, lamv[:, 3, :], ALU.mult)
            P.op("dve", lambda e: e.reduce_sum(out=lamw[:, 0:2], in_=tmp[:], axis=AX.X), [tmp], [lamw])
            ACT(lamw, lamw[:, 2:4], lamw, lamw[:, 0:2], AF.Exp)
            TTo("dve", lamw, lamw[:, 4:5], lamw, lamw[:, 2:3], lamw, lamw[:, 3:4], ALU.subtract)
            TS("dve", lamw, lamw[:, 5:6], lamw, lamw[:, 4:5], lam_init, -1.0, ALU.add, ALU.mult)
            TS("dve", subw, subw[:], subw, subw[:], 1.0 - lam_init, None, ALU.mult)
            wb = [alloc([128, NK, 1024], BF16, "adaw%d" % i) for i in range(2)]
            pa = banks[0]
            ws = WS()
            for blk in range(18):
                wt = wb[blk % 2]
                for k0 in range(0, NK, 2):
                    ws.add(wt, wt[:, k0:k0 + 2, :], w["ada_w"][k0 * 128:(k0 + 2) * 128, blk * 1024:(blk + 1) * 1024].rearrange("(k p) n -> p k n", p=128), 2)
                ws.mark()
            for blk in range(18):
                wt = wb[blk % 2]
                ws.unit(blk)
                for mt in range(8):
                    j = blk * 8 + mt
                    for k in range(NK):
                        MM(pa, pa[:, 2 * j:2 * j + 2], wt, wt[:, k, mt * 128:(mt + 1) * 128], condb, condb[:, k, :], k == 0, k == NK - 1)
            TTo("dve", modT, modT[:], pa, pa[:, 0:288].rearrange("p (j i) -> p j i", i=2), adab,
                adab[:].unsqueeze(2).to_broadcast([128, 144, 2]), ALU.add)
            for s in range(3):
                TS("dve", Gs[s], Gs[s][:], modT, modT[:, (3 * s + 1) * NK:(3 * s + 2) * NK, :], 1.0, None, ALU.add)
                TTo("dve", Gs[s], Gs[s][:], Gs[s], Gs[s][:], normT, normT[:, s, :].unsqueeze(2).to_broadcast([128, NK, 2]), ALU.mult)
                TS("dve", GTg[s], GTg[s][:], modT, modT[:, (3 * s + 2) * NK:(3 * s + 3) * NK, :], 1.0 if s == 1 else 0.5, None, ALU.mult)

        def phase_ffn(li, s, Xin, Xout, final=False):
            w = W[li]
            slot = 0 if s == 0 else 2
            wgu, wd = w["wgu"][s], w["wd"][s]
            phase_reset()
            npt = PASS // TT
            u = alloc([128, NK, PASS], BF16, "u")
            hoff = A["off"]
            h = alloc([128, NJ, PASS], BF16, "h")
            hend = A["off"]
            A["off"] = hoff
            xts = [alloc([128, NK, TT], F32, "xt%d" % i) for i in range(1)] * 2
            sqb = alloc([128, NK, TT], BF16, "sqb")
            A["off"] = hend
            rs = alloc([128, TT], F32, "rs")
            rsd = alloc([128, TT], F32, "rsd")
            wg = [alloc([128, 2, NK, 128], BF16, "wg%d" % i) for i in range(2)]
            wdn = [alloc([128, NJ, 128], BF16, "wd%d" % i) for i in range(2)]
            sg = [alloc([128, TT], F32, "sg%d" % i) for i in range(2)]
            xm = [alloc([128, TT], F32, "xm%d" % i) for i in range(2)]
            xo = [alloc([128, TT], F32, "xo%d" % i) for i in range(2)]
            cnt = 0
            ws = WS()
            for p in range(T // PASS):
                for j in range(NJ):
                    for a_ in range(2):
                        ws.add(wg[j % 2], wg[j % 2][:, a_, :, :], wgu[j][:, a_ * 2048:(a_ + 1) * 2048], NK)
                    ws.mark()
                for m in range(NK):
                    for k0 in range(0, NJ, 16):
                        k1 = min(NJ, k0 + 16)
                        ws.add(wdn[m % 2], wdn[m % 2][:, k0:k1, :], wd[m][:, k0 * 128:k1 * 128], k1 - k0)
                    ws.mark()
            ui = 0
            for p in range(T // PASS):
                t0 = p * PASS
                P.barrier()
                for tt in range(npt):
                    g0 = t0 + tt * TT
                    xt = xts[tt % 2]
                    LD(xt, xt[:], xview(Xin)[:, :, g0:g0 + TT])
                    modnorm(xt, slot, g0, u, u[:, :, tt * TT:(tt + 1) * TT], sqb, rs, rsd, banks[6])
                P.barrier()
                for j in range(NJ):
                    wt = wg[j % 2]
                    ws.unit(ui)
                    ui += 1
                    for tt in range(npt):
                        gp, vp = banks[(cnt % 2) * 2], banks[(cnt % 2) * 2 + 1]
                        sgt = sg[cnt % 2]
                        cnt += 1
                        cs = slice(tt * TT, (tt + 1) * TT)
                        for k in range(NK):
                            MM(gp, gp[:, 0:TT], wt, wt[:, 0, k, :], u, u[:, k, cs], k == 0, k == NK - 1)
                        for k in range(NK):
                            MM(vp, vp[:, 0:TT], wt, wt[:, 1, k, :], u, u[:, k, cs], k == 0, k == NK - 1)
                        ACT(sgt, sgt[:], gp, gp[:, 0:TT], AF.Silu)
                        TTo("dve", h, h[:, j, cs], sgt, sgt[:], vp, vp[:, 0:TT], ALU.mult)
                for m in range(NK):
                    wt = wdn[m % 2]
                    ws.unit(ui)
                    ui += 1
                    for tt in range(npt):
                        g0 = t0 + tt * TT
                        op_ = banks[4 + cnt % 2]
                        xmt, xot = xm[cnt % 2], xo[cnt % 2]
                        cnt += 1
                        cs = slice(tt * TT, (tt + 1) * TT)
                        LD(xmt, xmt[:], Xin[m * 128:(m + 1) * 128, g0:g0 + TT])
                        for k in range(NJ):
                            MM(op_, op_[:, 0:TT], wt, wt[:, k, :], h, h[:, k, cs], k == 0, k == NJ - 1)
                        for (a, b, i) in segs(g0, g0 + TT):
                            STT("dve", xot, xot[:, a - g0:b - g0], op_, op_[:, a - g0:b - g0], GTg[slot][:, m, i:i + 1],
                                xmt, xmt[:, a - g0:b - g0], ALU.mult, ALU.add, extra=[GTg[slot]])
                        if not final:
                            STO(xot, Xout[m * 128:(m + 1) * 128, g0:g0 + TT], xot[:])
                        else:
                            for (a, b, i) in segs(g0, g0 + TT):
                                if i == 0:
                                    STO(xot, yT[m * 128:(m + 1) * 128, a:b], xot[:, a - g0:b - g0])

        def phase_proj(li, Xin):
            w = W[li]
            phase_reset()
            u = alloc([128, NK, T], BF16, "u")
            off0 = A["off"]
            xts = [alloc([128, NK, TT], F32, "xt%d" % i) for i in range(1)] * 2
            sqb = alloc([128, NK, TT], BF16, "sqb")
            rs = alloc([128, TT], F32, "rs")
            rsd = alloc([128, TT], F32, "rsd")
            for tt in range(T // TT):
                g0 = tt * TT
                xt = xts[tt % 2]
                LD(xt, xt[:], xview(Xin)[:, :, g0:g0 + TT])
                modnorm(xt, 1, g0, u, u[:, :, g0:g0 + TT], sqb, rs, rsd, banks[6])
            P.barrier()
            A["off"] = off0
            wbk = [alloc([128, NK, 512], BF16, "wb%d" % i) for i in range(2)]
            cosT = alloc([128, L], F32, "cosT")
            sinT = alloc([128, L], F32, "sinT")
            LD(cosT, cosT[:], cos_d[:, :])
            LD(sinT, sinT[:], sin_d[:, :])
            stg = [alloc([128, 512], F32, "stg%d" % i) for i in range(2)]
            stgb = [alloc([128, 512], BF16, "stgb%d" % i) for i in range(2)]
            qfs = [alloc([128, TT], F32, "qf%d" % i) for i in range(2)]
            qns = [alloc([128, TT], F32, "qn%d" % i) for i in range(2)]
            sq2s = [alloc([128, TT], BF16, "sq2%d" % i) for i in range(2)]
            r1s = [alloc([128, TT], F32, "r1%d" % i) for i in range(2)]
            r2s = [alloc([128, TT], F32, "r2%d" % i) for i in range(2)]
            t1s = [alloc([128, TT], F32, "t1%d" % i) for i in range(2)]
            t2s = [alloc([128, TT], F32, "t2%d" % i) for i in range(2)]
            st = {"n": 0, "b": 0}
            ws = WS()
            blocks = ([(i * 512, 512) for i in range(16)] + [(8192, 64)] + [(8256 + i * 512, 512) for i in range(8)])
            for bi, (c0, ncols) in enumerate(blocks):
                wt = wbk[bi % 2]
                for k0 in range(0, NK, 4):
                    ws.add(wt, wt[:, k0:k0 + 4, 0:ncols], w["w_in"][k0 * 128:(k0 + 4) * 128, c0:c0 + ncols].rearrange("(k p) n -> p k n", p=128), 4)
                ws.mark()

            def load_block(c0, ncols):
                bi = st["b"]
                assert blocks[bi] == (c0, ncols), (bi, c0, ncols)
                wt = wbk[bi % 2]
                ws.unit(bi)
                st["b"] += 1
                return wt

            def fm_tile(wt, mt, g0):
                bk = banks[st["n"] % 2]
                st["n"] += 1
                for k in range(NK):
                    MM(bk, bk[:, 0:TT], wt, wt[:, k, mt * 128:(mt + 1) * 128], u, u[:, k, g0:g0 + TT], k == 0, k == NK - 1)
                return bk

            def tm_tile(wt, c, ncols):
                bk = banks[st["n"] % 2]
                st["n"] += 1
                for k in range(NK):
                    MM(bk, bk[:, 0:ncols], u, u[:, k, c * 128:(c + 1) * 128], wt, wt[:, k, 0:ncols], k == 0, k == NK - 1)
                return bk

            tiles = [(which, dst, blk, mt, tt) for which, dst in ((0, QTs), (1, KTs)) for blk in range(2)
                     for mt in range(4) for tt in range(T // TT)]
            curw = {}
            tbank = {}

            def stageA(i):
                which, dst, blk, mt, tt = tiles[i]
                if mt == 0 and tt == 0:
                    curw["wt"] = load_block(which * 1024 + blk * 512, 512)
                pp = i % 2
                bk = fm_tile(curw["wt"], mt, tt * TT)
                ACT(qfs[pp], qfs[pp][:], bk, bk[:, 0:TT], AF.Copy)
                ACT(sq2s[pp], sq2s[pp][:], bk, bk[:, 0:TT], AF.Square)

            def stageB(i):
                which, dst, blk, mt, tt = tiles[i]
                hd, g0, pp = blk * 4 + mt, tt * TT, i % 2
                qf, qn, sq2, r1, r2, t1, t2 = qfs[pp], qns[pp], sq2s[pp], r1s[pp], r2s[pp], t1s[pp], t2s[pp]
                bss, brq = banks[2 + 2 * pp], banks[3 + 2 * pp]
                MM(bss, bss[:, 0:TT], blk1, blk1[:], sq2, sq2[:], True, True)
                ACT(r1, r1[:], bss, bss[:, 0:TT], AF.Sqrt, bias=EPS, scale=1.0 / 64)
                RECIP(r2, r2[:], r1, r1[:])
                STT("dve", qn, qn[:], qf, qf[:], qkn[:, which:which + 1], r2, r2[:], ALU.mult, ALU.mult, extra=[qkn])
                ob = stgb[pp]
                for (a, b, i_) in segs(g0, g0 + TT):
                    la, lb = a - g0, b - g0
                    if i_ == 0:
                        MM(brq, brq[:, la:lb], rmat, rmat[:], qn, qn[:, la:lb], True, True)
                        TTo("pool", t1, t1[:, la:lb], qn, qn[:, la:lb], cosT, cosT[:, a:b], ALU.mult)
                        TTo("dve", t2, t2[:, la:lb], brq, brq[:, la:lb], sinT, sinT[:, a:b], ALU.mult)
                        TTo("dve", ob, ob[:, la:lb], t1, t1[:, la:lb], t2, t2[:, la:lb], ALU.add)
                    else:
                        CP("dve", ob, ob[:, la:lb], qn, qn[:, la:lb])
                STO(ob, dst[hd, :, g0:g0 + TT], ob[:, 0:TT])

            stageA(0)
            for i in range(len(tiles)):
                if i + 1 < len(tiles):
                    stageA(i + 1)
                stageB(i)
            for blk in range(2):
                wt = load_block(2048 + blk * 512, 512)
                for c in range(NCH):
                    bk = tm_tile(wt, c, 512)
                    ob = stgb[st["n"] % 2]
                    ACT(ob, ob[:], bk, bk[:], AF.Copy)
                    STO(ob, Vtok[c * 128:(c + 1) * 128, blk * 512:(blk + 1) * 512], ob[:])
            for blk in range(4):
                wt = load_block(3072 + blk * 512, 512)
                for c in range(NCH):
                    bk = tm_tile(wt, c, 512)
                    ob = stg[st["n"] % 2]
                    ACT(ob, ob[:], bk, bk[:], AF.Silu)
                    STO(ob, SZ[c * 128:(c + 1) * 128, blk * 512:(blk + 1) * 512], ob[:])
            for blk in range(6):
                wt = load_block(5120 + blk * 512, 512)
                for mt in range(4):
                    for tt in range(T // TT):
                        g0 = tt * TT
                        bk = fm_tile(wt, mt, g0)
                        ob = stg[st["n"] % 2]
                        ACT(ob, ob[:, 0:TT], bk, bk[:, 0:TT], AF.Copy)
                        STO(ob, XBC[(blk * 4 + mt) * 128:(blk * 4 + mt + 1) * 128, g0:g0 + TT], ob[:, 0:TT])
            wt = load_block(8192, 64)
            for c in range(NCH):
                bk = tm_tile(wt, c, 64)
                ob = stg[st["n"] % 2]
                TTo("dve", ob, ob[:, 128:192], bk, bk[:, 0:64], dtb, dtb[:], ALU.add)
                ACT(ob, ob[:, 192:256], ob, ob[:, 128:192], AF.Exp)
                ACT(ob, ob[:, 0:64], ob, ob[:, 192:256], AF.Ln, bias=1.0)
                TTo("dve", ob, ob[:, 64:128], ob, ob[:, 0:64], aneg, aneg[:], ALU.mult)
                STO(ob, DTDA[c * 128:(c + 1) * 128, :], ob[:, 0:128])
            for blk in range(8):
                wt = load_block(8256 + blk * 512, 512)
                for mt in range(4):
                    for tt in range(T // TT):
                        g0 = tt * TT
                        bk = fm_tile(wt, mt, g0)
                        ob = stg[st["n"] % 2]
                        ACT(ob, ob[:, 0:TT], bk, bk[:, 0:TT], AF.Sigmoid)
                        STO(ob, GTs[(blk * 4 + mt) * 128:(blk * 4 + mt + 1) * 128, g0:g0 + TT], ob[:, 0:TT])

        def phase_conv(li):
            phase_reset()
            xc = [alloc([128, T], F32, "xc%d" % i) for i in range(2)]
            acc = [alloc([128, T], F32, "acc%d" % i) for i in range(2)]
            ys = [alloc([128, T], F32, "ys%d" % i) for i in range(2)]
            ysb = [alloc([128, T], BF16, "ysb%d" % i) for i in range(2)]
            tst = [alloc([128, 4, 128], F32, "tst%d" % i) for i in range(2)]
            tsb = [alloc([128, 4, 128], BF16, "tsb%d" % i) for i in range(2)]
            n = 0
            for ch in range(24):
                x_, a_, y_, yb_ = xc[ch % 2], acc[ch % 2], ys[ch % 2], ysb[ch % 2]
                LD(x_, x_[:], XBC[ch * 128:(ch + 1) * 128, :])
                for (sa, sb_) in ((0, L), (L, T)):
                    TS("dve", a_, a_[:, sa:sb_], x_, x_[:, sa:sb_], convw[:, ch, 2:3], convb[:, ch:ch + 1], ALU.mult, ALU.add, extra=[convw, convb])
                    for j in (0, 1, 3, 4):
                        o = j - 2
                        ta, tb = max(sa, sa - o), min(sb_, sb_ - o)
                        STT("dve", a_, a_[:, ta:tb], x_, x_[:, ta + o:tb + o], convw[:, ch, j:j + 1], a_, a_[:, ta:tb], ALU.mult, ALU.add, extra=[convw])
                ACT(y_, y_[:], a_, a_[:], AF.Silu)
                if ch >= 16:
                    CP("pool", yb_, yb_[:], y_, y_[:])
                    g = (ch - 16) % 4
                    STO(yb_, (BTs if ch < 20 else CTs)[g, :, :], yb_[:])
                if ch < 20:
                    for c0 in range(0, NCH, 4):
                        ncg = min(4, NCH - c0)
                        bk = banks[n % 2]
                        for cc in range(ncg):
                            c = c0 + cc
                            TR(bk, bk[:, cc * 128:(cc + 1) * 128], y_, y_[:, c * 128:(c + 1) * 128], ident, ident[:])
                        if ch < 16:
                            tb_ = tst[n % 2]
                            ACT(tb_, tb_[:, 0:ncg, :], bk, bk[:, 0:ncg * 128].rearrange("p (c f) -> p c f", f=128), AF.Copy)
                            STO(tb_, XStok.rearrange("(c p) f -> p c f", p=128)[:, c0:c0 + ncg, ch * 128:(ch + 1) * 128], tb_[:, 0:ncg, :])
                        else:
                            tb_ = tsb[n % 2]
                            ACT(tb_, tb_[:, 0:ncg, :], bk, bk[:, 0:ncg * 128].rearrange("p (c f) -> p c f", f=128), AF.Copy)
                            STO(tb_, Btok.rearrange("(c p) f -> p c f", p=128)[:, c0:c0 + ncg, (ch - 16) * 128:(ch - 15) * 128], tb_[:, 0:ncg, :])
                        n += 1

        def phase_att(li):
            phase_reset()
            kt = [alloc([128, T], BF16, "kt%d" % i) for i in range(2)]
            qt = [alloc([128, T], BF16, "qt%d" % i) for i in range(2)]
            vt = [alloc([128, NCH, 128], BF16, "vt%d" % i) for i in range(2)]
            pe_ = [alloc([128, QT_], BF16, "p%d" % i) for i in range(4)]
            rr = [alloc([128, QT_], F32, "rr%d" % i) for i in range(2)]
            o0 = alloc([128, QT_], F32, "o0")
            o1 = alloc([128, QT_], F32, "o1")
            osq = alloc([128, QT_], BF16, "osq")
            ob = [alloc([128, QT_], BF16, "ob%d" % i) for i in range(2)]
            n = 0
            pend_tail = []
            cn = [alloc([128, QT_], F32, "cn%d" % i) for i in range(4)]
            for hd in range(HEADS):
                k_, q_, v_ = kt[hd % 2], qt[hd % 2], vt[hd % 2]
                LD(k_, k_[:], KTs[hd, :, :])
                LD(q_, q_[:], QTs[hd, :, :])
                LD(v_, v_[:], Vtok.rearrange("(c p) f -> p c f", p=128)[:, :, hd * 128:(hd + 1) * 128])
                qtiles = [(q0, q0 + QT_, list(range(NCH))) for q0 in range(0, L, QT_)]
                qtiles += [(L, T, list(range(L // 128, NCH)))]
                for (qa, qb, kcs) in qtiles:
                    nq = qb - qa
                    num0, num1, den0, den1 = banks[4], banks[5], banks[6], banks[7]
                    def scores(kc):
                        nonlocal n
                        ks = slice(kc * 128, (kc + 1) * 128)
                        s0, s1 = banks[(n % 2) * 2], banks[(n % 2) * 2 + 1]
                        p0, p1 = pe_[(n % 2) * 2], pe_[(n % 2) * 2 + 1]
                        n += 1
                        MM(s0, s0[:, 0:nq], k_, k_[0:64, ks], q_, q_[0:64, qa:qb], True, True)
                        MM(s1, s1[:, 0:nq], k_, k_[64:128, ks], q_, q_[64:128, qa:qb], True, True)
                        ACT(p0, p0[:, 0:nq], s0, s0[:, 0:nq], AF.Exp, scale=0.125)
                        ACT(p1, p1[:, 0:nq], s1, s1[:, 0:nq], AF.Exp, scale=0.125)
                        return p0, p1

                    pend = scores(kcs[0])
                    for ki, kc in enumerate(kcs):
                        first, last = ki == 0, ki == len(kcs) - 1
                        p0, p1 = pend
                        if not last:
                            pend = scores(kcs[ki + 1])
                        if pend_tail and (ki == min(3, len(kcs) - 1)):
                            pend_tail.pop()()
                        MM(num0, num0[:, 0:nq], v_, v_[:, kc, :], p0, p0[:, 0:nq], first, last)
                        MM(den0, den0[:, 0:nq], onesb, onesb[:], p0, p0[:, 0:nq], first, last)
                        MM(num1, num1[:, 0:nq], v_, v_[:, kc, :], p1, p1[:, 0:nq], first, last)
                        MM(den1, den1[:, 0:nq], onesb, onesb[:], p1, p1[:, 0:nq], first, last)
                    ACT(cn[0], cn[0][:, 0:nq], num0, num0[:, 0:nq], AF.Copy)
                    CP("dve", cn[2], cn[2][:, 0:nq], den0, den0[:, 0:nq])
                    ACT(cn[1], cn[1][:, 0:nq], num1, num1[:, 0:nq], AF.Copy)
                    CP("dve", cn[3], cn[3][:, 0:nq], den1, den1[:, 0:nq])
                    RECIP(rr[0], rr[0][:, 0:nq], cn[2], cn[2][:, 0:nq])
                    RECIP(rr[1], rr[1][:, 0:nq], cn[3], cn[3][:, 0:nq])
                    TTo("dve", o0, o0[:, 0:nq], cn[0], cn[0][:, 0:nq], rr[0], rr[0][:, 0:nq], ALU.mult)
                    TTo("pool", o1, o1[:, 0:nq], cn[1], cn[1][:, 0:nq], rr[1], rr[1][:, 0:nq], ALU.mult)
                    STT("dve", o0, o0[:, 0:nq], o1, o1[:, 0:nq], lamw[:, 5:6], o0, o0[:, 0:nq], ALU.mult, ALU.add, extra=[lamw])
                    ACT(osq, osq[:, 0:nq], o0, o0[:, 0:nq], AF.Square)

                    def tail(nq=nq, qa=qa, qb=qb, hd=hd):
                        sb_ = banks[0]
                        MM(sb_, sb_[:, 0:nq], onesb, onesb[:], osq, osq[:, 0:nq], True, True)
                        ACT(rr[0], rr[0][:, 0:nq], sb_, sb_[:, 0:nq], AF.Sqrt, bias=EPS, scale=1.0 / 128)
                        RECIP(rr[1], rr[1][:, 0:nq], rr[0], rr[0][:, 0:nq])
                        o_ = ob[hd % 2]
                        STT("dve", o_, o_[:, 0:nq], o0, o0[:, 0:nq], subw[:, 0:1], rr[1], rr[1][:, 0:nq], ALU.mult, ALU.mult, extra=[subw])
                        STO(o_, OATT[hd * 128:(hd + 1) * 128, qa:qb], o_[:, 0:nq])
                    pend_tail.append(tail)

            while pend_tail:
                pend_tail.pop()()

        def phase_ssm(li):
            w = W[li]
            phase_reset()
            nlat = L // 128
            fwd = list(range(nlat, NCH)) + list(range(nlat))
            bwd = list(range(NCH - 1, nlat - 1, -1)) + list(range(nlat - 1, -1, -1))
            hsave = alloc([128, NCH, D], BF16, "hsave")
            hf = alloc([128, D], F32, "hf")
            hb = alloc([128, D], F32, "hb")
            hbb = alloc([128, D], BF16, "hbb")
            dskip = alloc([128, D], F32, "dskip")
            ssmn = alloc([128, D], F32, "ssmn")
            LD(dskip, dskip[:], w["dskip"][:, :])
            LD(ssmn, ssmn[:], w["ssmn"][:, :])
            xs = [alloc([128, D], F32, "xs%d" % i) for i in range(1)] * 2
            dtda = [alloc([128, 128], F32, "dtda%d" % i) for i in range(2)]
            btk = [alloc([128, 512], BF16, "btk%d" % i) for i in range(2)]
            btf = [alloc([128, 4, 128], BF16, "btf%d" % i) for i in range(2)]
            ctf = [alloc([128, 4, 128], BF16, "ctf%d" % i) for i in range(2)]
            sz = [alloc([128, D], F32, "sz%d" % i) for i in range(1)] * 2
            wc = alloc([128, 64], F32, "wc")
            ea = alloc([128, 64], F32, "ea")
            dw = alloc([128, 64], F32, "dw")
            xdw = alloc([128, D], BF16, "xdw")
            xdt = [alloc([128, D], BF16, "xdt%d" % i) for i in range(2)]
            cbm = [alloc([128, 4, 128], F32, "cbm%d" % i) for i in range(2)]
            LT = [alloc([128, 4, 128], F32, "LT%d" % i) for i in range(2)]
            RB = [alloc([128, 4, 128], F32, "RB%d" % i) for i in range(2)]
            E = [alloc([128, 4, 128], F32, "E%d" % i) for i in range(2)]
            EA = [alloc([128, 4, 128], F32, "EA%d" % i) for i in range(2)]
            MT = [alloc([128, 4, 128], BF16, "MT%d" % i) for i in range(2)]
            CE = [alloc([128, 4, 128], BF16, "CE%d" % i) for i in range(2)]
            yt = alloc([128, 512], F32, "yt")
            gz = alloc([128, 512], F32, "gz")
            junk = alloc([128, 512], F32, "junk")
            ssq = alloc([128, 4], F32, "ssq")
            nacol = alloc([128, 64], F32, "nacol")
            ob = [alloc([128, 512], BF16, "ob%d" % i) for i in range(2)]
            ot = [alloc([128, 4, 128], BF16, "ot%d" % i) for i in range(2)]
            identb = alloc([128, 128], BF16, "identb")
            CP("dve", identb, identb[:], ident, ident[:])
            P.op("dve", lambda e: e.memset(hf[:], 0.0), [], [hf])
            P.op("dve", lambda e: e.memset(hb[:], 0.0), [], [hb])
            P.op("pool", lambda e: e.memset(hbb[:], 0.0), [], [hbb])
            MU = {0: (0, 1), 1: (2, 3)}
            cbank, sbank = banks[0], banks[1]

            def tokv(Xd, c):
                return Xd[c * 128:(c + 1) * 128, :]

            def state_update(d, c_i, x_, dd_, bt_, hst):
                U = masks[:, MU[d][0], :]
                da = dd_[:, 64 + d * 32:64 + (d + 1) * 32]
                MM(cbank, cbank[:, 0:32], masks, U, dd_, da, True, True)
                MM(cbank, cbank[:, 32:64], onesf, onesf[:], dd_, da, True, True)
                ACT(wc, wc[:, 0:32], cbank, cbank[:, 0:32], AF.Exp)
                ACT(ea, ea[:, 0:32], cbank, cbank[:, 32:64], AF.Exp)
                TTo("dve", dw, dw[:, 0:32], wc, wc[:, 0:32], dd_, dd_[:, d * 32:(d + 1) * 32], ALU.mult)
                TTo("dve", xdw, xdw[:].rearrange("p (r q) -> p r q", q=64), x_, x_[:].rearrange("p (r q) -> p r q", q=64),
                    dw, dw[:, 0:32].unsqueeze(2).to_broadcast([128, 32, 64]), ALU.mult)
                for g in range(4):
                    MM(sbank, sbank[:], bt_, bt_[:, g * 128:(g + 1) * 128], xdw, xdw[:, g * 512:(g + 1) * 512], True, True)
                    hv = hst[:, g * 512:(g + 1) * 512].rearrange("p (r q) -> p r q", q=64)
                    TTo("dve", hst, hv, hst, hv, ea, ea[:, g * 8:(g + 1) * 8].unsqueeze(2).to_broadcast([128, 8, 64]), ALU.mult)
                    TTo("dve", hst, hst[:, g * 512:(g + 1) * 512], hst, hst[:, g * 512:(g + 1) * 512], sbank, sbank[:], ALU.add)

            for i, c in enumerate(fwd):
                x_, dd_, bt_ = xs[i % 2], dtda[i % 2], btk[i % 2]
                LD(x_, x_[:], tokv(XStok, c))
                LD(dd_, dd_[:], tokv(DTDA, c))
                LD(bt_, bt_[:], tokv(Btok, c))
                ACT(hsave, hsave[:, c, :], hf, hf[:], AF.Copy)
                if i < NCH - 1:
                    state_update(0, c, x_, dd_, bt_, hf)
            n = 0
            for i, c in enumerate(bwd):
                x_, dd_, bt_, sz_ = xs[i % 2], dtda[i % 2], btk[i % 2], sz[i % 2]
                bf_, cf_ = btf[i % 2], ctf[i % 2]
                LD(x_, x_[:], tokv(XStok, c))
                LD(dd_, dd_[:], tokv(DTDA, c))
                LD(bt_, bt_[:], tokv(Btok, c))
                LD(sz_, sz_[:], tokv(SZ, c))
                LD(bf_, bf_[:], BTs[:, :, c * 128:(c + 1) * 128].rearrange("g p t -> p g t"))
                LD(cf_, cf_[:], CTs[:, :, c * 128:(c + 1) * 128].rearrange("g p t -> p g t"))
                for g in range(4):
                    MM(cbank, cbank[:, g * 128:(g + 1) * 128], bf_, bf_[:, g, :], cf_, cf_[:, g, :], True, True)
                for d in range(2):
                    TTo("dve", cbm[d], cbm[d][:], cbank, cbank[:].rearrange("p (g l) -> p g l", l=128), masks,
                        masks[:, 4 + d, :].unsqueeze(1).to_broadcast([128, 4, 128]), ALU.mult)
                    TTo("pool", xdt[d], xdt[d][:].rearrange("p (r q) -> p r q", q=64), x_, x_[:].rearrange("p (r q) -> p r q", q=64),
                        dd_, dd_[:, d * 32:(d + 1) * 32].unsqueeze(2).to_broadcast([128, 32, 64]), ALU.mult)
                for d in range(2):
                    MM(sbank, sbank[:, d * 32:(d + 1) * 32], masks, masks[:, MU[d][1], :], dd_, dd_[:, 64 + d * 32:64 + (d + 1) * 32], True, True)
                ACT(nacol, nacol[:], sbank, sbank[:, 0:64], AF.Identity, scale=-1.0)
                units = [(g, d, hb_) for g in range(4) for d in range(2) for hb_ in range(2)]
                ycnt = {}

                def front(ui, k):
                    g, d, hb_ = units[ui]
                    rb_, bcb = RB[k % 2], banks[4 + k % 2]
                    h0 = g * 8 + hb_ * 4
                    tri = masks[:, MU[d][1], :]
                    dac = dd_[:, 64 + d * 32 + h0:64 + d * 32 + h0 + 4].unsqueeze(2).to_broadcast([128, 4, 128])
                    TTo("pool", rb_, rb_[:], masks, tri.unsqueeze(1).to_broadcast([128, 4, 128]), dd_, dac, ALU.mult)
                    MM(bcb, bcb[:], onesf, onesf[:], rb_, rb_[:].rearrange("p r l -> p (r l)"), True, True)

                def back(ui, k):
                    g, d, hb_ = units[ui]
                    e_, ea_, mt_, ce_, bcb = E[k % 2], EA[k % 2], MT[k % 2], CE[k % 2], banks[4 + k % 2]
                    ybank = banks[6 + g % 2]
                    h0 = g * 8 + hb_ * 4
                    ACT(ea_, ea_[:].rearrange("p r l -> p (r l)"), bcb, bcb[:], AF.Exp)
                    for r4 in range(4):
                        ACT(e_, e_[:, r4, :], bcb, bcb[:, r4 * 128:(r4 + 1) * 128], AF.Exp,
                            bias=nacol[:, d * 32 + h0 + r4:d * 32 + h0 + r4 + 1], extra=[nacol])
                    STT("dve", mt_, mt_[:], e_, e_[:], 1.0, cbm[d], cbm[d][:, g, :].unsqueeze(1).to_broadcast([128, 4, 128]), ALU.min, ALU.mult)
                    TTo("pool", ce_, ce_[:], ea_, ea_[:], cf_, cf_[:, g, :].unsqueeze(1).to_broadcast([128, 4, 128]), ALU.mult)
                    for r4 in range(4):
                        hh = h0 + r4
                        r = hb_ * 4 + r4
                        ys_ = ybank[:, r * 64:(r + 1) * 64]
                        hs_b, hs_ap = (hsave, hsave[:, c, hh * 64:(hh + 1) * 64]) if d == 0 else (hbb, hbb[:, hh * 64:(hh + 1) * 64])
                        cn = ycnt.get(g, 0)
                        MM(ybank, ys_, mt_, mt_[:, r4, :], xdt[d], xdt[d][:, hh * 64:(hh + 1) * 64], cn == 0, False)
                        MM(ybank, ys_, ce_, ce_[:, r4, :], hs_b, hs_ap, False, cn == 30)
                        ycnt[g] = cn + 2

                def evac(g):
                        ybank = banks[6 + g % 2]
                        gs = slice(g * 512, (g + 1) * 512)
                        TTo("pool", yt, yt[:], x_, x_[:, gs], dskip, dskip[:, gs], ALU.mult)
                        TTo("dve", yt, yt[:], yt, yt[:], ybank, ybank[:], ALU.add)
                        TTo("dve", gz, gz[:], yt, yt[:], sz_, sz_[:, gs], ALU.mult)
                        P.op("dve", lambda e: e.memset(ssq[:, 0:1], 0.0), [], [ssq])
                        ACT(junk, junk[:], gz, gz[:], AF.Square, accum=ssq[:, 0:1], accb=ssq)
                        ACT(ssq, ssq[:, 1:2], ssq, ssq[:, 0:1], AF.Sqrt, bias=EPS, scale=1.0 / 512)
                        RECIP(ssq, ssq[:, 2:3], ssq, ssq[:, 1:2])
                        o_ = ob[g % 2]
                        STT("dve", o_, o_[:], gz, gz[:], ssq[:, 2:3], ssmn, ssmn[:, gs], ALU.mult, ALU.mult, extra=[ssq])
                        tbk = banks[0] if False else banks[1]
                        tv = tbk[:].bitcast(BF16)
                        for q in range(4):
                            TR(tbk, tv[:, q * 128:(q + 1) * 128], o_, o_[:, q * 128:(q + 1) * 128], identb, identb[:])
                        ot_ = ot[g % 2]
                        CP("dve", ot_, ot_[:], tbk, tv[:, 0:512].rearrange("p (q l) -> p q l", l=128))
                        STO(ot_, OSSM[g * 512:(g + 1) * 512, c * 128:(c + 1) * 128].rearrange("(q p) l -> p q l", p=128), ot_[:])

                front(0, n)
                for ui in range(16):
                    if ui + 1 < 16:
                        front(ui + 1, n + ui + 1)
                    back(ui, n + ui)
                    if ui % 4 == 3:
                        evac(units[ui][0])
                n += 16
                if i < NCH - 1:
                    state_update(1, c, x_, dd_, bt_, hb)
                    CP("pool", hbb, hbb[:], hb, hb[:])

        def phase_merge(li, Xin, Xout):
            w = W[li]
            phase_reset()
            MPASS = min(1152, T)
            assert T % MPASS == 0
            npt = MPASS // TT
            oa = alloc([128, 8, MPASS], BF16, "oa")
            os_ = alloc([128, NK, MPASS], BF16, "os")
            mm = alloc([128, NK, MPASS], BF16, "mm")
            wa = [alloc([128, 8, 128], BF16, "wa%d" % i) for i in range(2)]
            ws = [alloc([128, NK, 128], BF16, "ws%d" % i) for i in range(2)]
            wo = [alloc([128, NK, 128], BF16, "wo%d" % i) for i in range(2)]
            ga = [alloc([128, TT], F32, "ga%d" % i) for i in range(2)]
            gs_ = [alloc([128, TT], F32, "gs%d" % i) for i in range(2)]
            t1 = [alloc([128, TT], F32, "t1%d" % i) for i in range(2)]
            t2 = [alloc([128, TT], F32, "t2%d" % i) for i in range(2)]
            xm = [alloc([128, TT], F32, "xm%d" % i) for i in range(2)]
            xo = [alloc([128, TT], F32, "xo%d" % i) for i in range(2)]
            cnt = 0
            wst = WS()
            for p in range(T // MPASS):
                for m in range(NK):
                    wst.add(wa[m % 2], wa[m % 2][:], w["w_ba"][m], 8)
                    wst.add(ws[m % 2], ws[m % 2][:], w["w_bs"][m], NK)
                    wst.mark()
                for m in range(NK):
                    wst.add(wo[m % 2], wo[m % 2][:], w["w_out"][m], NK)
                    wst.mark()
            ui = 0
            for p in range(T // MPASS):
                t0 = p * MPASS
                LD(oa, oa[:], OATT.rearrange("(k p) t -> p k t", p=128)[:, :, t0:t0 + MPASS])
                LD(os_, os_[:], OSSM.rearrange("(k p) t -> p k t", p=128)[:, :, t0:t0 + MPASS])
                for m in range(NK):
                    wa_, ws_ = wa[m % 2], ws[m % 2]
                    wst.unit(ui)
                    ui += 1
                    for tt in range(npt):
                        g0 = t0 + tt * TT
                        cs = slice(tt * TT, (tt + 1) * TT)
                        pa, ps_ = banks[(cnt % 2) * 2], banks[(cnt % 2) * 2 + 1]
                        ga_, gs2, a1, a2 = ga[cnt % 2], gs_[cnt % 2], t1[cnt % 2], t2[cnt % 2]
                        cnt += 1
                        LD(ga_, ga_[:], GTs[m * 128:(m + 1) * 128, g0:g0 + TT])
                        LD(gs2, gs2[:], GTs[D + m * 128:D + (m + 1) * 128, g0:g0 + TT])
                        for k in range(8):
                            MM(pa, pa[:, 0:TT], wa_, wa_[:, k, :], oa, oa[:, k, cs], k == 0, k == 7)
                        for k in range(NK):
                            MM(ps_, ps_[:, 0:TT], ws_, ws_[:, k, :], os_, os_[:, k, cs], k == 0, k == NK - 1)
                        TTo("dve", a1, a1[:], pa, pa[:, 0:TT], ga_, ga_[:], ALU.mult)
                        TTo("dve", a2, a2[:], ps_, ps_[:, 0:TT], gs2, gs2[:], ALU.mult)
                        TTo("pool", mm, mm[:, m, cs], a1, a1[:], a2, a2[:], ALU.add)
                for m in range(NK):
                    wo_ = wo[m % 2]
                    wst.unit(ui)
                    ui += 1
                    for tt in range(npt):
                        g0 = t0 + tt * TT
                        cs = slice(tt * TT, (tt + 1) * TT)
                        po = banks[4 + cnt % 2]
                        xmt, xot = xm[cnt % 2], xo[cnt % 2]
                        cnt += 1
                        LD(xmt, xmt[:], Xin[m * 128:(m + 1) * 128, g0:g0 + TT])
                        for k in range(NK):
                            MM(po, po[:, 0:TT], wo_, wo_[:, k, :], mm, mm[:, k, cs], k == 0, k == NK - 1)
                        for (a, b, i) in segs(g0, g0 + TT):
                            STT("dve", xot, xot[:, a - g0:b - g0], po, po[:, a - g0:b - g0], GTg[1][:, m, i:i + 1],
                                xmt, xmt[:, a - g0:b - g0], ALU.mult, ALU.add, extra=[GTg[1]])
                        STO(xot, Xout[m * 128:(m + 1) * 128, g0:g0 + TT], xot[:])

        cur = xT
        for li in range(nlayers):
            last = li == nlayers - 1
            phase_prep(li)
            phase_ffn(li, 0, cur, XA if cur is not XA else XB)
            cur = XA if cur is not XA else XB
            phase_proj(li, cur)
            phase_conv(li)
            phase_att(li)
            phase_ssm(li)
            nxt = XB if cur is XA else XA
            phase_merge(li, cur, nxt)
            cur = nxt
            nxt = XB if cur is XA else XA
            phase_ffn(li, 1, cur, nxt, final=last)
            cur = nxt
        bfs = dict(QTs=QTs, KTs=KTs, Vtok=Vtok, Btok=Btok, BTs=BTs, CTs=CTs, OATT=OATT, OSSM=OSSM)
        for name in debug:
            if name in bfs:
                src = bfs[name]
                if len(src.shape) == 3:
                    src = src.rearrange("h p t -> (h p) t")
                rows, cols = src.shape
                dst = nc.dram_tensor(name + "_f", [rows, cols], F32, kind="ExternalOutput").ap()
                phase_reset()
                tb = alloc([128, cols], BF16, "dbgb")
                tf = alloc([128, cols], F32, "dbgf")
                for r0 in range(0, rows, 128):
                    LD(tb, tb[:], src[r0:r0 + 128, :])
                    CP("dve", tf, tf[:], tb, tb[:])
                    STO(tf, dst[r0:r0 + 128, :], tf[:])
        P.barrier()
        P.finalize()
    return nc


def _blk(wm):
    K, M = wm.shape
    return np.ascontiguousarray(wm.reshape(K // 128, 128, M // 128, 128).transpose(2, 1, 0, 3)).reshape(M // 128, 128, (K // 128) * 128)


def _col(v):
    n = v.shape[0] // 128
    return np.ascontiguousarray(v.reshape(n, 128).T)


def _rope_tables(L):
    n_freq = 16
    inv = (10000.0 ** (-np.arange(n_freq, dtype=np.float32) / n_freq)).astype(np.float32)
    rows = L // 64
    row = np.repeat(np.arange(rows, dtype=np.float32), 64)
    col = np.tile(np.arange(64, dtype=np.float32), rows)
    ang = np.concatenate([row[:, None] * inv, col[:, None] * inv], axis=-1).astype(np.float32)
    cos, sin = np.cos(ang).astype(np.float32), np.sin(ang).astype(np.float32)
    idx = (np.arange(128) % 64) // 2
    return np.ascontiguousarray(cos[:, idx].T), np.ascontiguousarray(sin[:, idx].T)


def _constants(L):
    j = np.arange(128)[:, None]
    s = np.arange(128)[None, :]
    masks = np.stack([(j > s), (j <= s), (j < s), (j >= s), (s >= j), (s <= j)], axis=1).astype(np.float32)
    blockones = ((j // 64) == (s // 64)).astype(np.float32)
    rmat = np.zeros((128, 128), np.float32)
    for i in range(64):
        rmat[2 * i + 1, 2 * i] = -1.0
        rmat[2 * i, 2 * i + 1] = 1.0
    cosT, sinT = _rope_tables(L)
    return dict(ident=np.eye(128, dtype=np.float32), masks=np.ascontiguousarray(masks), blockones=blockones,
                rmat=rmat, cosT=cosT, sinT=sinT)


def prep_shared(inp, L, nlayers=2):
    f = lambda a: np.asarray(a, dtype=np.float32)
    sh = _constants(L)
    for li in range(nlayers):
        s = str(li)
        sh["ada_w" + s] = f(inp["ada_w"][li])
        sh["ada_bT" + s] = _col(f(inp["ada_b"][li]))
        sh["normT" + s] = np.ascontiguousarray(np.stack([_col(f(inp[k][li])) for k in ("ffn1_norm", "mix_norm", "ffn2_norm")], axis=1))
        for n_, (kg, kd) in enumerate((("ffn1_w_gu", "ffn1_w_down"), ("ffn2_w_gu", "ffn2_w_down"))):
            wgu = f(inp[kg][li])
            a = wgu.reshape(NK, 128, 2, NJ, 128).transpose(3, 1, 2, 0, 4)
            sh["wgu%d_%s" % (n_ + 1, s)] = np.ascontiguousarray(a).reshape(NJ, 128, 2 * NK * 128)
            sh["wd%d_%s" % (n_ + 1, s)] = _blk(f(inp[kd][li]))
        sh["w_in" + s] = f(inp["w_in"][li])
        qn, kn = f(inp["q_norm"][li]), f(inp["k_norm"][li])
        sh["qkn" + s] = np.ascontiguousarray(np.stack([np.tile(qn, 2), np.tile(kn, 2)], axis=1))
        lv = np.stack([f(inp[k][li]) for k in ("lambda_q1", "lambda_k1", "lambda_q2", "lambda_k2")], axis=0)
        sh["lamv" + s] = np.ascontiguousarray(np.broadcast_to(lv[None], (128, 4, 64)))
        sh["subln" + s] = f(inp["attn_subln"][li]).reshape(128, 1).copy()
        cw = f(inp["conv_w"][li])
        sh["convw" + s] = np.ascontiguousarray(cw.reshape(5, 24, 128).transpose(2, 1, 0))
        sh["convb" + s] = _col(f(inp["conv_b"][li]))
        sh["dtb" + s] = np.ascontiguousarray(np.broadcast_to(f(inp["dt_bias"][li]).reshape(1, 64), (128, 64)))
        sh["alog" + s] = np.ascontiguousarray(np.broadcast_to(f(inp["a_log"][li]).reshape(1, 64), (128, 64)))
        sh["dskip" + s] = np.ascontiguousarray(np.broadcast_to(np.repeat(f(inp["d_skip"][li]), 64)[None], (128, D)))
        sh["ssmn" + s] = np.ascontiguousarray(np.broadcast_to(f(inp["ssm_norm"][li])[None], (128, D)))
        sh["w_ba" + s] = _blk(f(inp["w_branch_attn"][li]))
        sh["w_bs" + s] = _blk(f(inp["w_branch_ssm"][li]))
        sh["w_out" + s] = _blk(f(inp["w_out"][li]))
    return sh


def prep_core(inp, b):
    x, ctx = np.asarray(inp["x"][b], np.float32), np.asarray(inp["ctx"][b], np.float32)
    xT = np.ascontiguousarray(np.concatenate([x, ctx], axis=0).T)
    c = np.asarray(inp["c"][b], np.float32)
    cc = np.asarray(inp["c_ctx"], np.float32)
    condT = np.ascontiguousarray(np.stack([_col(c), _col(cc)], axis=2))
    return dict(xT=xT, condT=condT)


def kernel(**inputs):
    B, L, _ = inputs["x"].shape
    LC = inputs["ctx"].shape[1]
    nc = build(L, LC, 2)
    sh = prep_shared(inputs, L, 2)
    in_maps = []
    for b in range(B):
        m = dict(sh)
        m.update(prep_core(inputs, b))
        in_maps.append(m)
    res = run_bass_kernel_spmd(nc, in_maps, core_ids=list(range(B)))
    out = np.stack([np.ascontiguousarray(r["yT"].T) for r in res.results], axis=0)
    return out.astype(np.float32)
```
